# Optimizing a Trainium2 kernel written in Bass

```python
import math
import jax, jax.numpy as jnp
from jax import lax
import numpy as np

D_MODEL = 1024
BATCH = 4
SEQ = 8192
DEPTH = 1
DEC_BATCH = 8
DEC_SEQ = 8192
PAST_LEN = 128

HEAD_DIM = 64
N_DIFF_HEADS = 4
DIFF_V_DIM = 2 * HEAD_DIM
N_DIL_HEADS = 8
DIL_PATTERNS = ((128, 1), (512, 4), (2048, 16))
D_FF = 4 * D_MODEL
Q_BLOCK = 128
EPS = 1e-5
DIFF_QK_W = N_DIFF_HEADS * 2 * HEAD_DIM
DIFF_V_W = N_DIFF_HEADS * DIFF_V_DIM
DIL_W = N_DIL_HEADS * HEAD_DIM
IN_W = 2 * DIFF_QK_W + DIFF_V_W + 3 * DIL_W
MIX_W = DIFF_V_W + DIL_W

kernel_name = "hymba_diff_dilated_encoder"


def rmsnorm(x, g):
    xf = x.astype(jnp.float32)
    xf = xf * lax.rsqrt(jnp.mean(xf * xf, axis=-1, keepdims=True) + EPS)
    return (xf * g.astype(jnp.float32)).astype(x.dtype)


def alibi_slopes(n):
    return 2.0 ** (-8.0 * jnp.arange(1, n + 1, dtype=jnp.float32) / n)


def diff_attention(q, k, v, lam, slopes):
    B, S, H, _, dh = q.shape
    scale = dh ** -0.5
    nq = S // Q_BLOCK
    qb = q.reshape(B, nq, Q_BLOCK, H, 2, dh).transpose(1, 0, 2, 3, 4, 5)
    q0 = jnp.arange(nq, dtype=jnp.int32) * Q_BLOCK
    pos_k = jnp.arange(S, dtype=jnp.float32)

    def block(args):
        qblk, start = args
        s = jnp.einsum('bqhmd,bkhmd->bhmqk', qblk, k,
                       preferred_element_type=jnp.float32) * scale
        pos_q = (start + jnp.arange(Q_BLOCK, dtype=jnp.int32)).astype(jnp.float32)
        dist = jnp.abs(pos_q[:, None] - pos_k[None, :])
        s = s - slopes[:, None, None, None] * dist
        p = jax.nn.softmax(s, axis=-1)
        w = p[:, :, 0] - lam * p[:, :, 1]
        return jnp.einsum('bhqk,bkhe->bqhe', w.astype(v.dtype), v,
                          preferred_element_type=jnp.float32)

    o = lax.map(block, (qb, q0))
    return o.transpose(1, 0, 2, 3, 4).reshape(B, S, H, v.shape[-1])


def banded_dilated(q, k, v, dilation, radius, slopes):
    B, H, S, dh = q.shape
    d, r = dilation, radius
    L = S // d
    nb = -(-L // r)
    Lp = nb * r
    scale = dh ** -0.5

    def to_sub(t):
        return t.reshape(B, H, L, d, dh).transpose(0, 1, 3, 2, 4)

    qs = jnp.pad(to_sub(q), ((0, 0), (0, 0), (0, 0), (0, Lp - L), (0, 0)))
    pad_kv = ((0, 0), (0, 0), (0, 0), (r, Lp - L + r), (0, 0))
    ks = jnp.pad(to_sub(k), pad_kv).reshape(B, H, d, nb + 2, r, dh)
    vs = jnp.pad(to_sub(v), pad_kv).reshape(B, H, d, nb + 2, r, dh)
    qs = qs.reshape(B, H, d, nb, r, dh)
    kwin = jnp.concatenate([ks[:, :, :, :-2], ks[:, :, :, 1:-1], ks[:, :, :, 2:]], axis=4)
    vwin = jnp.concatenate([vs[:, :, :, :-2], vs[:, :, :, 1:-1], vs[:, :, :, 2:]], axis=4)

    blk = jnp.arange(nb, dtype=jnp.int32)[:, None] * r
    i_idx = blk + jnp.arange(r, dtype=jnp.int32)[None, :]
    j_idx = blk - r + jnp.arange(3 * r, dtype=jnp.int32)[None, :]
    rel = jnp.abs(i_idx[:, :, None] - j_idx[:, None, :])
    valid = (j_idx[:, None, :] >= 0) & (j_idx[:, None, :] < L) & (rel <= r)
    dist = (rel * d).astype(jnp.float32)

    s = jnp.einsum('bhgnqd,bhgnkd->bhgnqk', qs, kwin,
                   preferred_element_type=jnp.float32) * scale
    s = s - slopes[:, None, None, None, None] * dist
    s = jnp.where(valid, s, -jnp.inf)
    m = jnp.max(s, axis=-1, keepdims=True)
    e = jnp.exp(s - m)
    den = jnp.sum(e, axis=-1, keepdims=True)
    p = e / den
    lse = (m + jnp.log(den))[..., 0]
    o = jnp.einsum('bhgnqk,bhgnkd->bhgnqd', p.astype(v.dtype), vwin,
                   preferred_element_type=jnp.float32)
    o = o.reshape(B, H, d, Lp, dh)[:, :, :, :L].transpose(0, 1, 3, 2, 4).reshape(B, H, S, dh)
    lse = lse.reshape(B, H, d, Lp)[:, :, :, :L].transpose(0, 1, 3, 2).reshape(B, H, S)
    return o, lse


def dilated_mixture(q, k, v, slopes):
    B, S, H, dh = q.shape
    qt, kt, vt = (t.transpose(0, 2, 1, 3) for t in (q, k, v))
    outs, lses = [], []
    for window, dil in DIL_PATTERNS:
        o, lse = banded_dilated(qt, kt, vt, dil, window // (2 * dil), slopes)
        outs.append(o)
        lses.append(lse)
    wts = jax.nn.softmax(jnp.stack(lses, axis=0), axis=0)
    out = jnp.sum(wts[..., None] * jnp.stack(outs, axis=0), axis=0)
    return out.transpose(0, 2, 1, 3).reshape(B, S, H * dh)


def encoder_layer(x, g_mix, w_in, lam_qk, g_diff, g_dil, w_out, g_mlp, w_up, w_down, lam_init):
    B, S, _ = x.shape
    h = rmsnorm(x, g_mix)
    proj = h @ w_in
    splits = np.cumsum([DIFF_QK_W, DIFF_QK_W, DIFF_V_W, DIL_W, DIL_W]).tolist()
    qa, ka, va, qb, kb, vb = jnp.split(proj, splits, axis=-1)

    lq = lam_qk.astype(jnp.float32)
    lam = jnp.exp(jnp.dot(lq[0], lq[1])) - jnp.exp(jnp.dot(lq[2], lq[3])) + lam_init
    oa = diff_attention(qa.reshape(B, S, N_DIFF_HEADS, 2, HEAD_DIM),
                        ka.reshape(B, S, N_DIFF_HEADS, 2, HEAD_DIM),
                        va.reshape(B, S, N_DIFF_HEADS, DIFF_V_DIM),
                        lam, alibi_slopes(N_DIFF_HEADS))
    oa = oa * lax.rsqrt(jnp.mean(oa * oa, axis=-1, keepdims=True) + EPS)
    oa = (oa * g_diff.astype(jnp.float32) * (1.0 - lam_init)).reshape(B, S, DIFF_V_W)

    ob = dilated_mixture(qb.reshape(B, S, N_DIL_HEADS, HEAD_DIM),
                         kb.reshape(B, S, N_DIL_HEADS, HEAD_DIM),
                         vb.reshape(B, S, N_DIL_HEADS, HEAD_DIM),
                         alibi_slopes(N_DIL_HEADS))
    ob = rmsnorm(ob, g_dil)

    mix = jnp.concatenate([oa, ob], axis=-1).astype(x.dtype)
    x = x + mix @ w_out
    h = rmsnorm(x, g_mlp)
    x = x + jnp.square(jax.nn.relu(h @ w_up)) @ w_down
    return x


def trunk(x, g_mix, w_in, lam_qk, g_diff, g_dil, w_out, g_mlp, w_up, w_down, g_final):
    for l in range(DEPTH):
        lam_init = 0.8 - 0.6 * math.exp(-0.3 * l)
        x = encoder_layer(x, g_mix[l], w_in[l], lam_qk[l], g_diff[l], g_dil[l],
                          w_out[l], g_mlp[l], w_up[l], w_down[l], lam_init)
    return rmsnorm(x, g_final)


def setup_inputs(seed: int = 0) -> dict:
    key = jax.random.key(seed)
    ks = jax.random.split(key, 12)
    f32 = jnp.float32

    def gain(k, shape):
        return 1.0 + 0.02 * jax.random.normal(k, shape, f32)

    return {
        "x_prompt": jax.random.normal(ks[0], (BATCH, SEQ, D_MODEL), f32),
        "x_sample": jax.random.normal(ks[1], (DEC_BATCH, DEC_SEQ, D_MODEL), f32),
        "g_mix": gain(ks[2], (DEPTH, D_MODEL)),
        "w_in": jax.random.normal(ks[3], (DEPTH, D_MODEL, IN_W), f32) * D_MODEL ** -0.5,
        "lam_qk": 0.1 * jax.random.normal(ks[4], (DEPTH, 4, HEAD_DIM), f32),
        "g_diff": gain(ks[5], (DEPTH, N_DIFF_HEADS, DIFF_V_DIM)),
        "g_dil": gain(ks[6], (DEPTH, DIL_W)),
        "w_out": jax.random.normal(ks[7], (DEPTH, MIX_W, D_MODEL), f32) * MIX_W ** -0.5,
        "g_mlp": gain(ks[8], (DEPTH, D_MODEL)),
        "w_up": jax.random.normal(ks[9], (DEPTH, D_MODEL, D_FF), f32) * D_MODEL ** -0.5,
        "w_down": jax.random.normal(ks[10], (DEPTH, D_FF, D_MODEL), f32) * D_FF ** -0.5,
        "g_final": gain(ks[11], (D_MODEL,)),
    }


def reference(x_prompt, x_sample, g_mix, w_in, lam_qk, g_diff, g_dil, w_out,
              g_mlp, w_up, w_down, g_final):
    y_prompt = trunk(x_prompt, g_mix, w_in, lam_qk, g_diff, g_dil, w_out,
                     g_mlp, w_up, w_down, g_final)
    y_sample = trunk(x_sample, g_mix, w_in, lam_qk, g_diff, g_dil, w_out,
                     g_mlp, w_up, w_down, g_final)
    return (y_prompt, y_sample)
```

```python
import numpy as np
import ml_dtypes
from contextlib import ExitStack
import concourse.bass as bass
import concourse.mybir as mybir
from concourse.bass_utils import run_bass_kernel_spmd

F32 = mybir.dt.float32
BF16 = mybir.dt.bfloat16
AF = mybir.ActivationFunctionType
ALU = mybir.AluOpType
AX = mybir.AxisListType

DM = 1024
DFF = 4096
INW = 3072
EPS = 1e-5
SLOPE_A = [2.0 ** (-8.0 * (i + 1) / 4) for i in range(4)]
SLOPE_B = [2.0 ** (-8.0 * (i + 1) / 8) for i in range(8)]
import os
PATS = tuple(int(a) for a in os.environ.get("DBG_PATS", "1,4,16").split(","))
DBG = os.environ.get("DBG", "")
LAM_INIT = 0.2
NEG = -30000.0

SEQ = 8192
N_CORES = 8


def host_consts(S):
    nkt = S // 128
    c = {}
    c["ident"] = np.eye(128, dtype=np.float32).astype(ml_dtypes.bfloat16)
    cb = np.zeros((128, 4 * nkt), np.float32)
    for h in range(4):
        cb[:, h * nkt:(h + 1) * nkt] = -SLOPE_A[h] * 128.0 * np.arange(nkt, dtype=np.float32)[None, :]
    c["cb"] = cb
    k = np.arange(128, dtype=np.float32)[:, None]
    q = np.arange(512, dtype=np.float32)[None, :]
    c["dtl"] = np.concatenate([-np.abs(q - (k + 128.0 * t)) for t in range(4)], axis=1).astype(np.float32)
    u = np.arange(512)
    qaug = np.zeros((4, 2, 3, 512), np.float32)
    kaug = np.zeros((4, 3, S), np.float32)
    for h in range(4):
        s = SLOPE_A[h]
        lo = np.stack([-s * (u % 256), -s * 256.0 * (u // 256), np.ones(512)], 0)
        qaug[h, 0] = lo
        qaug[h, 1] = -lo
        kaug[h] = np.stack([np.ones(S), np.ones(S), s * (np.arange(S) % 128)], 0)
    c["qaug"] = qaug.astype(ml_dtypes.bfloat16)
    c["kaug"] = kaug.astype(ml_dtypes.bfloat16)
    qq = np.arange(384, dtype=np.float32)[None, :]
    ad = np.abs(qq - k - 128.0)
    c["dist"] = (-ad).astype(np.float32)
    c["mask"] = np.where(ad <= 64, 0.0, NEG).astype(np.float32)
    return c


def build_program(S, NQS, phases="ABCD"):
    NT = len(NQS)
    NTOK = NT * S
    NQT = sum(NQS)
    NKT = S // 128
    QOFF = [sum(NQS[:t]) for t in range(NT)]
    nc = bass.Bass("TRN2", target_bir_lowering=False)

    def din(name, shape, dt=F32):
        return nc.dram_tensor(name, list(shape), dt, kind="ExternalInput")

    x_t = din("x", [NTOK, DM]); x = x_t.ap()
    w_in = din("w_in", [DM, INW]).ap()
    w_out = din("w_out", [DM, DM]).ap()
    w_up = din("w_up", [DM, DFF]).ap()
    w_down = din("w_down", [DFF, DM]).ap()
    g_mix_t = din("g_mix", [1, DM])
    g_mlp_t = din("g_mlp", [1, DM])
    g_fin_t = din("g_final", [1, DM])
    g_diff_t = din("g_diff", [4, 128])
    g_dil_t = din("g_dil", [1, 512])
    lam_t = din("lam_qk", [1, 256])
    ident_d = din("ident", [128, 128], BF16).ap()
    cb_d = din("cb", [128, 4 * NKT]).ap()
    dtl_d = din("dtl", [128, 2048]).ap()
    qaug_d = din("qaug", [4, 2, 3, 512], BF16).ap()
    kaug_d = din("kaug", [4, 3, S], BF16).ap()
    dist_d = din("dist", [128, 384]).ap()
    mask_d = din("mask", [128, 384]).ap()
    y = nc.dram_tensor("y", [NQT, DM], F32, kind="ExternalOutput").ap()

    qaT = nc.dram_tensor("qaT", [512, NTOK], BF16).ap()
    kaT = nc.dram_tensor("kaT", [512, NTOK], BF16).ap()
    qbT = nc.dram_tensor("qbT", [512, NTOK], BF16).ap()
    kbT = nc.dram_tensor("kbT", [512, NTOK], BF16).ap()
    va = nc.dram_tensor("va", [NTOK, 512], BF16).ap()
    vb = nc.dram_tensor("vb", [NTOK, 512], BF16).ap()
    mixT = nc.dram_tensor("mixT", [1024, NQT], BF16).ap()

    def bcast(th, n):
        return bass.AP(th, 0, [[0, 128], [1, n]])

    def phase_A():
        ntile = NTOK // 512
        groups = []
        for i4 in range(4):
            groups.append(("fm", i4 * 128, qaT, i4 * 128, 0.125))
        for i4 in range(4):
            groups.append(("fm", 512 + i4 * 128, kaT, i4 * 128, 1.0))
        for i4 in range(4):
            groups.append(("fm", 1536 + i4 * 128, qbT, i4 * 128, 0.125))
        for i4 in range(4):
            groups.append(("fm", 2048 + i4 * 128, kbT, i4 * 128, 1.0))
        for j in range(4):
            groups.append(("tm", 1024, va, j, 1.0))
        for j in range(4):
            groups.append(("tm", 2560, vb, j, 1.0))
        NG = len(groups)
        with ExitStack() as es:
            e = es.enter_context
            wbf = e(nc.sbuf_tensor("a_wbf", [128, 8, INW], BF16))
            wst = [e(nc.sbuf_tensor(f"a_wst{i}", [128, INW], F32)) for i in range(2)]
            xt = [e(nc.sbuf_tensor(f"a_xt{i}", [128, 4, DM], F32)) for i in range(2)]
            junk = e(nc.sbuf_tensor("a_junk", [128, DM], BF16))
            ssq = [e(nc.sbuf_tensor(f"a_ssq{i}", [128, 4], F32)) for i in range(2)]
            std = [e(nc.sbuf_tensor(f"a_std{i}", [128, 4], F32)) for i in range(2)]
            rstd = [e(nc.sbuf_tensor(f"a_rstd{i}", [128, 4], F32)) for i in range(2)]
            hb = e(nc.sbuf_tensor("a_hb", [128, 4, DM], BF16))
            hT = [e(nc.sbuf_tensor(f"a_hT{i}", [128, 8, 512], BF16)) for i in range(2)]
            gbc = e(nc.sbuf_tensor("a_gbc", [128, DM], F32))
            ident = e(nc.sbuf_tensor("a_ident", [128, 128], BF16))
            stg = [e(nc.sbuf_tensor(f"a_stg{i}", [128, 512], BF16)) for i in range(4)]
            ptr = [e(nc.psum_tensor(f"a_ptr{i}", [128, 512], BF16)) for i in range(2)]
            pm = [e(nc.psum_tensor(f"a_pm{i}", [128, 512], F32)) for i in range(4)]
            s_w = [e(nc.semaphore(f"a_w{i}")) for i in range(2)]; s_wc = [e(nc.semaphore(f"a_wc{i}")) for i in range(2)]
            s_c = e(nc.semaphore("a_c"))
            s_x = [e(nc.semaphore(f"a_x{i}")) for i in range(2)]; s_a1 = e(nc.semaphore("a_a1")); s_std = e(nc.semaphore("a_std"))
            s_d1 = e(nc.semaphore("a_d1")); s_h = e(nc.semaphore("a_h"))
            s_trc = e(nc.semaphore("a_trc")); s_tre = e(nc.semaphore("a_tre"))
            s_mm = e(nc.semaphore("a_mm")); s_ev = [e(nc.semaphore(f"a_ev{i}")) for i in range(2)]
            s_st = [e(nc.semaphore(f"a_st{i}")) for i in range(4)]
            block = e(nc.Block())

            @block.sync
            def _(sp):
                sp.dma_start(out=gbc[:], in_=bcast(g_mix_t, DM)).then_inc(s_c, 16)
                sp.dma_start(out=ident[:], in_=ident_d).then_inc(s_c, 16)
                for c in range(8):
                    if c >= 2:
                        sp.wait_ge(s_wc[c % 2], c // 2)
                    sp.dma_start(out=wst[c % 2][:], in_=w_in[c * 128:(c + 1) * 128, :]).then_inc(s_w[c % 2], 16)
                for i in range(ntile):
                    if i >= 2:
                        sp.wait_ge(s_h, i - 1)
                    sp.dma_start(out=xt[i % 2][:], in_=x[i * 512:(i + 1) * 512, :].rearrange("(j p) f -> p j f", p=128)).then_inc(s_x[i % 2], 16)

            @block.scalar
            def _(act):
                for c in range(0, 8, 2):
                    act.wait_ge(s_w[0], 16 * (c // 2 + 1))
                    act.activation(out=wbf[:, c, :], in_=wst[0][:], func=AF.Copy).then_inc(s_wc[0], 1)
                def norm_stats(i2):
                    p2 = i2 % 2
                    act.wait_ge(s_x[p2], 16 * (i2 // 2 + 1))
                    for j in range(4):
                        act.activation(out=junk[:], in_=xt[p2][:, j, :], func=AF.Square,
                                       accum_out=ssq[p2][:, j:j + 1]).then_inc(s_a1, 1)
                        act.wait_ge(s_a1, 4 * i2 + j + 1)
                    act.activation(out=std[p2][:], in_=ssq[p2][:], func=AF.Sqrt, bias=EPS, scale=1.0 / DM).then_inc(s_std, 1)

                norm_stats(0)
                for i in range(ntile):
                    p = i % 2
                    for c in range(8):
                        act.wait_ge(s_trc, 8 * i + c + 1)
                        if i >= 2 and c == 0:
                            act.wait_ge(s_mm, NG * (i - 1))
                        act.activation(out=hT[p][:, c, :], in_=ptr[c % 2][:], func=AF.Copy).then_inc(s_tre, 1)
                    if i + 1 < ntile:
                        norm_stats(i + 1)
                    for n in range(NG):
                        G = NG * i + n
                        if G % 2 != 0:
                            continue
                        kind, col0, dst, idx, scale = groups[n]
                        act.wait_ge(s_mm, G + 1)
                        if G >= 4:
                            act.wait_ge(s_st[G % 4], 16 * (G // 4))
                        act.activation(out=stg[G % 4][:], in_=pm[G % 4][:], func=AF.Copy, scale=scale).then_inc(s_ev[0], 1)

            @block.vector
            def _(v):
                for c in range(1, 8, 2):
                    v.wait_ge(s_w[1], 16 * (c // 2 + 1))
                    v.tensor_copy(out=wbf[:, c, :], in_=wst[1][:]).then_inc(s_wc[1], 1)
                v.wait_ge(s_c, 32)
                def hstage(i2):
                    p2 = i2 % 2
                    v.wait_ge(s_std, i2 + 1)
                    v.reciprocal(out=rstd[p2][:], in_=std[p2][:]).then_inc(s_d1, 1)
                    v.wait_ge(s_d1, i2 + 1)
                    if i2 >= 1:
                        v.wait_ge(s_trc, 8 * i2)
                    for j in range(4):
                        ins = v.scalar_tensor_tensor(out=hb[:, j, :], in0=xt[p2][:, j, :], scalar=rstd[p2][:, j:j + 1],
                                                     in1=gbc[:], op0=ALU.mult, op1=ALU.mult)
                    ins.then_inc(s_h, 1)

                hstage(0)
                for i in range(ntile):
                    p = i % 2
                    if i + 1 < ntile:
                        hstage(i + 1)
                    for n in range(NG):
                        G = NG * i + n
                        if G % 2 != 1:
                            continue
                        kind, col0, dst, idx, scale = groups[n]
                        v.wait_ge(s_mm, G + 1)
                        if G >= 4:
                            v.wait_ge(s_st[G % 4], 16 * (G // 4))
                        if scale != 1.0:
                            v.tensor_scalar(out=stg[G % 4][:], in0=pm[G % 4][:], scalar1=scale, scalar2=None,
                                            op0=ALU.mult).then_inc(s_ev[1], 1)
                        else:
                            v.tensor_copy(out=stg[G % 4][:], in_=pm[G % 4][:]).then_inc(s_ev[1], 1)

            @block.tensor
            def _(pe):
                pe.wait_ge(s_wc[0], 4)
                pe.wait_ge(s_wc[1], 4)
                pe.wait_ge(s_c, 32)
                for i in range(ntile):
                    p = i % 2
                    pe.wait_ge(s_h, i + 1)
                    for c in range(8):
                        T = 8 * i + c
                        if T >= 2:
                            pe.wait_ge(s_tre, T - 1)
                        for j in range(4):
                            ins = pe.transpose(ptr[c % 2][:, j * 128:(j + 1) * 128], hb[:, j, c * 128:(c + 1) * 128], ident[:])
                        ins.then_inc(s_trc, 1)
                    pe.wait_ge(s_tre, 8 * (i + 1))
                    for n in range(NG):
                        G = NG * i + n
                        kind, col0, dst, idx, scale = groups[n]
                        if G >= 4:
                            Gp = G - 4
                            pe.wait_ge(s_ev[Gp % 2], Gp // 2 + 1)
                        for c in range(8):
                            if kind == "fm":
                                ins = pe.matmul(pm[G % 4][:], wbf[:, c, col0:col0 + 128], hT[p][:, c, :],
                                                start=(c == 0), stop=(c == 7))
                            else:
                                ins = pe.matmul(pm[G % 4][:], hT[p][:, c, idx * 128:(idx + 1) * 128],
                                                wbf[:, c, col0:col0 + 512], start=(c == 0), stop=(c == 7))
                        ins.then_inc(s_mm, 1)

            @block.gpsimd
            def _(g):
                for i in range(ntile):
                    for n in range(NG):
                        G = NG * i + n
                        kind, col0, dst, idx, scale = groups[n]
                        g.wait_ge(s_ev[G % 2], G // 2 + 1)
                        if kind == "fm":
                            o = dst[idx:idx + 128, i * 512:(i + 1) * 512]
                        else:
                            r0 = i * 512 + idx * 128
                            o = dst[r0:r0 + 128, :]
                        g.dma_start(out=o, in_=stg[G % 4][:]).then_inc(s_st[G % 4], 16)
                tot = NG * ntile
                for k in range(4):
                    g.wait_ge(s_st[k], 16 * ((tot - k + 3) // 4))

    def phase_B():
        ths = [(t, h) for t in range(NT) for h in range(4)]
        blocks = []
        for t, h in ths:
            for qb in range(NQS[t] // 512):
                blocks.append((t, h, qb))
        first_of_th = {}
        for B, (t, h, qb) in enumerate(blocks):
            first_of_th.setdefault((t, h), B)
        NB = len(blocks)
        NKV = 2 + 2 + 4 + 8

        def kind_of(qb, kt):
            if kt < 4 * qb:
                return "lo"
            if kt > 4 * qb + 3:
                return "up"
            return "dg"

        def ndiag_before(B):
            return 4 * B

        with ExitStack() as es:
            e = es.enter_context
            K2 = [[e(nc.sbuf_tensor(f"b_K{p}{m}", [67, S], BF16)) for m in range(2)] for p in range(2)]
            V2 = [e(nc.sbuf_tensor(f"b_V{p}", [128, NKT, 128], BF16)) for p in range(2)]
            Qlo = [[e(nc.sbuf_tensor(f"b_Qlo{p}{m}", [67, 512], BF16)) for m in range(2)] for p in range(2)]
            Qup = [[e(nc.sbuf_tensor(f"b_Qup{p}{m}", [67, 512], BF16)) for m in range(2)] for p in range(2)]
            NP = 3
            P = [e(nc.sbuf_tensor(f"b_P{p}", [128, 1024], BF16)) for p in range(NP)]
            Sb = [e(nc.sbuf_tensor(f"b_Sb{p}", [128, 1024], F32)) for p in range(2)]
            Lacc = e(nc.sbuf_tensor("b_Lacc", [128, 512], F32))
            Lhi = e(nc.sbuf_tensor("b_Lhi", [128, 512], BF16))
            Llo = e(nc.sbuf_tensor("b_Llo", [128, 512], BF16))
            dtl = e(nc.sbuf_tensor("b_dtl", [128, 2048], F32))
            cb = e(nc.sbuf_tensor("b_cb", [128, 4 * NKT], F32))
            ones = e(nc.sbuf_tensor("b_ones", [128, 128], BF16))
            lq = e(nc.sbuf_tensor("b_lq", [128, 256], F32))
            ltmp = e(nc.sbuf_tensor("b_ltmp", [128, 128], F32))
            ldot = e(nc.sbuf_tensor("b_ldot", [128, 2], F32))
            lexp = e(nc.sbuf_tensor("b_lexp", [128, 2], F32))
            lamneg = e(nc.sbuf_tensor("b_lamneg", [128, 1], F32))
            r0 = e(nc.sbuf_tensor("b_r0", [128, 512], F32)); r1 = e(nc.sbuf_tensor("b_r1", [128, 512], F32))
            t0 = e(nc.sbuf_tensor("b_t0", [128, 512], F32)); t1 = e(nc.sbuf_tensor("b_t1", [128, 512], F32))
            ob = [e(nc.sbuf_tensor(f"b_ob{i}", [128, 512], BF16)) for i in range(2)]
            Sps = [e(nc.psum_tensor(f"b_S{p}", [128, 1024], F32)) for p in range(2)]
            O = [e(nc.psum_tensor(f"b_O{m}", [128, 512], F32)) for m in range(2)]
            L = [e(nc.psum_tensor(f"b_L{m}", [128, 512], F32)) for m in range(2)]
            s_c = e(nc.semaphore("b_c")); s_kv = [e(nc.semaphore(f"b_kv{i}")) for i in range(2)]; s_qa = e(nc.semaphore("b_qa")); s_q = [e(nc.semaphore(f"b_q{i}")) for i in range(2)]
            s_S = e(nc.semaphore("b_S")); s_P = e(nc.semaphore("b_P")); s_V = e(nc.semaphore("b_V"))
            s_Bd = e(nc.semaphore("b_Bd")); s_E = e(nc.semaphore("b_E")); s_Eo = e(nc.semaphore("b_Eo"))
            s_ds = e(nc.semaphore("b_ds")); s_lam = e(nc.semaphore("b_lam")); s_la = e(nc.semaphore("b_la"))
            s_ost = [e(nc.semaphore(f"b_ost{i}")) for i in range(2)]
            s_L = e(nc.semaphore("b_L")); s_hl = e(nc.semaphore("b_hl")); s_Lb = e(nc.semaphore("b_Lb"))
            s_E0 = e(nc.semaphore("b_E0"))
            block = e(nc.Block())

            def msl(m):
                return slice(m * 512, (m + 1) * 512)

            @block.sync
            def _(sp):
                sp.dma_start(out=dtl[:], in_=dtl_d).then_inc(s_c, 16)
                sp.dma_start(out=cb[:], in_=cb_d).then_inc(s_c, 16)
                sp.dma_start(out=lq[:], in_=bcast(lam_t, 256)).then_inc(s_c, 16)
                def load_kv(TH2):
                    t2, h2_ = ths[TH2]
                    pk = TH2 % 2
                    for m in range(2):
                        sp.dma_start(out=K2[pk][m][0:64, :], in_=kaT[h2_ * 128 + m * 64:h2_ * 128 + m * 64 + 64, t2 * S:(t2 + 1) * S]).then_inc(s_kv[pk], 16)
                        sp.dma_start(out=K2[pk][m][64:67, :], in_=kaug_d[h2_]).then_inc(s_kv[pk], 16)
                    nq4 = NKT // 4
                    for q4 in range(4):
                        src = va[t2 * S + q4 * nq4 * 128:t2 * S + (q4 + 1) * nq4 * 128, h2_ * 128:(h2_ + 1) * 128]
                        sp.dma_start(out=V2[pk][:, q4 * nq4:(q4 + 1) * nq4, :], in_=src.rearrange("(k p) e -> p k e", p=128)).then_inc(s_kv[pk], 16)

                load_kv(0)
                for B, (t, h, qb) in enumerate(blocks):
                    if first_of_th[(t, h)] == B:
                        TH = ths.index((t, h))
                        if B > 0:
                            sp.wait_ge(s_V, B * NKT)
                        for p in range(2):
                            for m in range(2):
                                sp.dma_start(out=Qlo[p][m][64:67, :], in_=qaug_d[h, 0]).then_inc(s_qa, 16)
                                sp.dma_start(out=Qup[p][m][64:67, :], in_=qaug_d[h, 1]).then_inc(s_qa, 16)
                        if TH + 1 < len(ths):
                            load_kv(TH + 1)
                    if B >= 2:
                        sp.wait_ge(s_S, (B - 1) * NKT)
                    for m in range(2):
                        src = qaT[h * 128 + m * 64:h * 128 + m * 64 + 64, t * S + qb * 512:t * S + (qb + 1) * 512]
                        sp.dma_start(out=Qlo[B % 2][m][0:64, :], in_=src).then_inc(s_q[B % 2], 16)
                        sp.dma_start(out=Qup[B % 2][m][0:64, :], in_=src).then_inc(s_q[B % 2], 16)

            @block.tensor
            def _(pe):
                pe.wait_ge(s_lam, 1)
                for B, (t, h, qb) in enumerate(blocks):
                    TH = ths.index((t, h))
                    K = K2[TH % 2]
                    V = V2[TH % 2]
                    if first_of_th[(t, h)] == B:
                        pe.wait_ge(s_kv[TH % 2], 16 * 8 * (TH // 2 + 1))
                        pe.wait_ge(s_qa, 16 * 8 * (TH + 1))
                    pe.wait_ge(s_q[B % 2], 64 * (B // 2 + 1))
                    g0 = B * NKT

                    def QK(kt):
                        g = g0 + kt
                        par = g % 2
                        if g >= 2:
                            pe.wait_ge(s_P, g - 1)
                        kd = kind_of(qb, kt)
                        for m in range(2):
                            if kd == "dg":
                                ins = pe.matmul(Sps[par][:, msl(m)], K[m][0:64, kt * 128:(kt + 1) * 128], Qlo[B % 2][m][0:64, :],
                                                start=True, stop=True)
                            else:
                                Q = Qlo if kd == "lo" else Qup
                                ins = pe.matmul(Sps[par][:, msl(m)], K[m][0:67, kt * 128:(kt + 1) * 128], Q[B % 2][m][0:67, :],
                                                start=True, stop=True)
                        ins.then_inc(s_S, 1)

                    def PV(kt):
                        g = g0 + kt
                        pe.wait_ge(s_P, g + 1)
                        if kt == 0 and B >= 1:
                            pe.wait_ge(s_E, B)
                        for m in range(2):
                            pe.matmul(O[m][:], V[:, kt, :], P[g % NP][:, msl(m)], start=(kt == 0), stop=(kt == NKT - 1))
                        ins = pe.matmul(L[1][:], ones[:], P[g % NP][:, msl(1)], start=(kt == 0), stop=(kt == NKT - 1))
                        ins.then_inc(s_V, 1)

                    def Lbcast(Bp):
                        pe.wait_ge(s_hl, Bp + 1)
                        if Bp >= 1:
                            pe.wait_ge(s_E0, Bp)
                        pe.matmul(L[0][:], ones[:], Lhi[:], start=True, stop=False)
                        pe.matmul(L[0][:], ones[:], Llo[:], start=False, stop=True).then_inc(s_Lb, 1)

                    QK(0)
                    if NKT > 1:
                        QK(1)
                    for kt in range(NKT):
                        PV(kt)
                        if kt + 2 < NKT:
                            QK(kt + 2)
                        if kt == 1 and B >= 1:
                            Lbcast(B - 1)
                    if B == NB - 1:
                        Lbcast(B)

            @block.scalar
            def _(act):
                act.wait_ge(s_lam, 2)
                act.activation(out=lexp[:], in_=ldot[:], func=AF.Exp).then_inc(s_la, 1)
                for B, (t, h, qb) in enumerate(blocks):
                    g0 = B * NKT
                    for kt in range(NKT):
                        g = g0 + kt
                        par = g % 2
                        kd = kind_of(qb, kt)
                        if kd == "dg":
                            act.wait_ge(s_Bd, ndiag_before(B) + (kt - 4 * qb) + 1)
                        else:
                            act.wait_ge(s_S, g + 1)
                        if g >= NP:
                            act.wait_ge(s_V, g - NP + 1)
                            act.wait_ge(s_L, g - NP + 1)
                        if kd == "dg":
                            ins = act.activation(out=P[g % NP][:], in_=Sb[par][:], func=AF.Exp)
                        else:
                            n = abs(kt - 4 * qb)
                            ins = act.activation(out=P[g % NP][:], in_=Sps[par][:], func=AF.Exp,
                                                 bias=cb[:, h * NKT + n:h * NKT + n + 1], scale=1.0)
                        ins.then_inc(s_P, 1)

            @block.vector
            def _(v):
                v.memset(ones[:], 1.0).then_inc(s_lam, 1)
                v.wait_ge(s_c, 48)
                v.tensor_tensor(out=ltmp[:, 0:64], in0=lq[:, 0:64], in1=lq[:, 64:128], op=ALU.mult)
                v.tensor_tensor(out=ltmp[:, 64:128], in0=lq[:, 128:192], in1=lq[:, 192:256], op=ALU.mult).then_inc(s_ds, 1)
                v.wait_ge(s_ds, 1)
                v.reduce_sum(out=ldot[:, 0:1], in_=ltmp[:, 0:64], axis=AX.X)
                v.reduce_sum(out=ldot[:, 1:2], in_=ltmp[:, 64:128], axis=AX.X).then_inc(s_lam, 1)
                v.wait_ge(s_la, 1)
                v.tensor_tensor(out=lamneg[:], in0=lexp[:, 1:2], in1=lexp[:, 0:1], op=ALU.subtract).then_inc(s_ds, 1)
                v.wait_ge(s_ds, 2)
                v.tensor_scalar(out=lamneg[:], in0=lamneg[:], scalar1=-LAM_INIT, scalar2=None, op0=ALU.add).then_inc(s_ds, 1)
                v.wait_ge(s_ds, 3)
                st_ = {"nds": 3}
                pending = []
                for B, (t, h, qb) in enumerate(blocks):
                    g0 = B * NKT

                    def biasadd(kt):
                        tt = kt - 4 * qb
                        g = g0 + kt
                        par = g % 2
                        v.wait_ge(s_S, g + 1)
                        if g >= 2:
                            v.wait_ge(s_P, g - 1)
                        for m in range(2):
                            ins = v.scalar_tensor_tensor(out=Sb[par][:, msl(m)], in0=dtl[:, tt * 512:(tt + 1) * 512],
                                                         scalar=float(SLOPE_A[h]), in1=Sps[par][:, msl(m)],
                                                         op0=ALU.mult, op1=ALU.add)
                        ins.then_inc(s_Bd, 1)

                    if kind_of(qb, 0) == "dg":
                        biasadd(0)
                    for kt in range(NKT):
                        g = g0 + kt
                        if kt + 1 < NKT and kind_of(qb, kt + 1) == "dg":
                            biasadd(kt + 1)
                        v.wait_ge(s_P, g + 1)
                        if kt == 0:
                            v.tensor_copy(out=Lacc[:], in_=P[g % NP][:, 0:512]).then_inc(s_L, 1)
                        else:
                            v.wait_ge(s_L, g)
                            v.tensor_tensor(out=Lacc[:], in0=P[g % NP][:, 0:512], in1=Lacc[:], op=ALU.add).then_inc(s_L, 1)
                        if pending and kt >= 4:
                            pending.pop(0)()
                    while pending:
                        pending.pop(0)()
                    v.wait_ge(s_V, g0 + NKT)
                    v.tensor_copy(out=r1[:], in_=L[1][:])
                    v.tensor_copy(out=t0[:], in_=O[0][:])
                    v.tensor_copy(out=t1[:], in_=O[1][:]).then_inc(s_E, 1)
                    v.wait_ge(s_E, B + 1)
                    v.wait_ge(s_L, g0 + NKT)
                    if B >= 1:
                        v.wait_ge(s_Lb, B)
                    v.tensor_copy(out=Lhi[:], in_=Lacc[:]).then_inc(s_ds, 1)
                    st_["nds"] += 1
                    v.wait_ge(s_ds, st_["nds"])
                    v.tensor_tensor(out=Llo[:], in0=Lacc[:], in1=Lhi[:], op=ALU.subtract).then_inc(s_hl, 1)
                    v.wait_ge(s_hl, B + 1)

                    def mk_tail(B=B):
                        def T1():
                            v.wait_ge(s_Lb, B + 1)
                            v.tensor_copy(out=r0[:], in_=L[0][:]).then_inc(s_E0, 1)
                        def T2():
                            v.wait_ge(s_E0, B + 1)
                            v.reciprocal(out=r0[:, 0:256], in_=r0[:, 0:256])
                        def T3():
                            v.reciprocal(out=r0[:, 256:512], in_=r0[:, 256:512])
                        def T4():
                            v.reciprocal(out=r1[:, 0:256], in_=r1[:, 0:256])
                        def T5():
                            v.reciprocal(out=r1[:, 256:512], in_=r1[:, 256:512]).then_inc(s_ds, 1)
                            st_["nds"] += 1
                        def T6():
                            v.wait_ge(s_ds, st_["nds"])
                            v.tensor_tensor(out=t0[:], in0=t0[:], in1=r0[:], op=ALU.mult)
                        def T7():
                            v.tensor_tensor(out=t1[:], in0=t1[:], in1=r1[:], op=ALU.mult).then_inc(s_ds, 1)
                            st_["nds"] += 1
                        def T8():
                            v.wait_ge(s_ds, st_["nds"])
                            if B >= 2:
                                v.wait_ge(s_ost[B % 2], 16 * (B // 2))
                            v.scalar_tensor_tensor(out=ob[B % 2][:], in0=t1[:], scalar=lamneg[:, 0:1], in1=t0[:],
                                                   op0=ALU.mult, op1=ALU.add).then_inc(s_Eo, 1)
                            v.wait_ge(s_Eo, B + 1)
                        return [T1, T2, T3, T4, T5, T6, T7, T8]

                    pending.extend(mk_tail())
                while pending:
                    pending.pop(0)()

            @block.gpsimd
            def _(g):
                for B, (t, h, qb) in enumerate(blocks):
                    g.wait_ge(s_Eo, B + 1)
                    c0 = QOFF[t] + qb * 512
                    g.dma_start(out=mixT[h * 128:(h + 1) * 128, c0:c0 + 512], in_=ob[B % 2][:]).then_inc(s_ost[B % 2], 16)
                for k in range(2):
                    g.wait_ge(s_ost[k], 16 * ((NB - k + 1) // 2))

    def phase_C():
        ths = [(t, h) for t in range(NT) for h in range(8)]
        units = {}
        ustart = {}
        U = 0
        for t, h in ths:
            lst = []
            for pi, d in enumerate(PATS):
                nt_ = S // d // 128
                nqb = NQS[t] // d // 128
                for g in range(d):
                    for m in range(min(nt_, nqb + 1)):
                        b0 = max(0, m - 1)
                        b1 = min(nqb - 1, m + 1)
                        lst.append((pi, d, g, m, nt_, 128 * b0, 128 * (b1 - b0 + 1)))
            units[(t, h)] = lst
            ustart[(t, h)] = U
            U += len(lst)
        NVD = [4 if d == 1 else d for d in PATS]
        pat_end = {}
        for th_ in ths:
            pe_ = {}
            for ui, (pi, d, g, m, nt_, qlo, W) in enumerate(units[th_]):
                pe_[pi] = ustart[th_] + ui + 1
            pat_end[th_] = pe_
        NQMAX = max(NQS)
        CH = 1024
        chunks = {}
        nch = 0
        for t, h in ths:
            l = []
            for c0 in range(0, NQS[t], CH):
                l.append((nch, c0, min(CH, NQS[t] - c0)))
                nch += 1
            chunks[(t, h)] = l

        with ExitStack() as es:
            e = es.enter_context
            Kh2 = [e(nc.sbuf_tensor(f"c_K{i}", [64, S], BF16)) for i in range(2)]
            Qh2 = [e(nc.sbuf_tensor(f"c_Q{i}", [64, S], BF16)) for i in range(2)]
            Vd = [e(nc.sbuf_tensor(f"c_V{i}", [128, NKT, 128], BF16)) for i in range(3)]
            ACC2 = [e(nc.sbuf_tensor(f"c_ACC{i}", [128, NQMAX], F32)) for i in range(2)]
            DEN = [e(nc.sbuf_tensor(f"c_DEN{i}", [64, CH], F32)) for i in range(2)]
            RD = e(nc.sbuf_tensor("c_RD", [64, CH], F32))
            outb = [e(nc.sbuf_tensor(f"c_outb{i}", [64, CH], BF16)) for i in range(2)]
            Bt = e(nc.sbuf_tensor("c_Bt", [128, 3, 384], BF16))
            identc = e(nc.sbuf_tensor("c_ident", [128, 128], BF16))
            dist = e(nc.sbuf_tensor("c_dist", [128, 384], F32))
            mask = e(nc.sbuf_tensor("c_mask", [128, 384], F32))
            NB = 3
            P = [e(nc.sbuf_tensor(f"c_P{i}", [128, 384], BF16)) for i in range(NB)]
            Sps = [e(nc.psum_tensor(f"c_S{i}", [128, 512], F32)) for i in range(NB)]
            Ops = [e(nc.psum_tensor(f"c_O{i}", [128, 512], F32)) for i in range(NB)]
            s_c = e(nc.semaphore("c_c"))
            s_kq = [e(nc.semaphore(f"c_kq{i}")) for i in range(2)]
            s_v = [e(nc.semaphore(f"c_v{i}")) for i in range(3)]
            s_S = e(nc.semaphore("c_S")); s_B = e(nc.semaphore("c_B")); s_P = e(nc.semaphore("c_P"))
            s_O = e(nc.semaphore("c_O")); s_A = e(nc.semaphore("c_A")); s_bt = e(nc.semaphore("c_bt"))
            s_one = e(nc.semaphore("c_one")); s_z = e(nc.semaphore("c_z"))
            s_den = [e(nc.semaphore(f"c_den{i}")) for i in range(2)]
            s_rd = e(nc.semaphore("c_rd")); s_F = e(nc.semaphore("c_F")); s_ln = e(nc.semaphore("c_ln"))
            s_ost = [e(nc.semaphore(f"c_ost{i}")) for i in range(2)]
            block = e(nc.Block())

            def qcols(d, g, b):
                st = 128 * b * d + g
                return slice(st, st + 127 * d + 1, d)

            def kcols(d, g, key0, n):
                st = key0 * d + g
                return slice(st, st + (n - 1) * d + 1, d)

            @block.sync
            def _(sp):
                sp.dma_start(out=dist[:], in_=dist_d).then_inc(s_c, 16)
                sp.dma_start(out=mask[:], in_=mask_d).then_inc(s_c, 16)
                sp.dma_start(out=identc[:], in_=ident_d).then_inc(s_c, 16)
                sp.wait_ge(s_one, 3)
                for TH, (t, h) in enumerate(ths):
                    if TH >= 2:
                        sp.wait_ge(s_S, ustart[ths[TH - 1]])
                    sp.dma_start(out=Kh2[TH % 2][:], in_=kbT[h * 64:(h + 1) * 64, t * S:(t + 1) * S]).then_inc(s_kq[TH % 2], 16)
                    sp.dma_start(out=Qh2[TH % 2][:, 0:NQS[t]], in_=qbT[h * 64:(h + 1) * 64, t * S:t * S + NQS[t]]).then_inc(s_kq[TH % 2], 16)
                    for pi, d in enumerate(PATS):
                        nt_ = S // d // 128
                        if TH >= 1:
                            sp.wait_ge(s_O, pat_end[ths[TH - 1]][pi])
                        if d == 1:
                            nq4 = nt_ // 4
                            for q4 in range(4):
                                src = vb[t * S + q4 * nq4 * 128:t * S + (q4 + 1) * nq4 * 128, h * 64:(h + 1) * 64]
                                sp.dma_start(out=Vd[pi][:, q4 * nq4:(q4 + 1) * nq4, 0:64],
                                             in_=src.rearrange("(k p) e -> p k e", p=128)).then_inc(s_v[pi], 16)
                        else:
                            for g in range(d):
                                src = vb[t * S:(t + 1) * S, h * 64:(h + 1) * 64].rearrange("(k p d) e -> d p k e", p=128, d=d)[g]
                                sp.dma_start(out=Vd[pi][:, g * nt_:(g + 1) * nt_, 0:64], in_=src).then_inc(s_v[pi], 16)

            @block.tensor
            def _(pe):
                for TH, (t, h) in enumerate(ths):
                    pe.wait_ge(s_kq[TH % 2], 32 * (TH // 2 + 1))
                    pe.wait_ge(s_bt, 3 * (TH + 1))
                    if TH == 0:
                        pe.wait_ge(s_c, 48)
                    lst = units[(t, h)]
                    U0 = ustart[(t, h)]
                    Kh = Kh2[TH % 2]
                    Qh = Qh2[TH % 2]
                    vwaited = set()

                    def QK(ui):
                        pi, d, g, m, nt_, qlo, W = lst[ui]
                        Ug = U0 + ui
                        if Ug >= NB:
                            pe.wait_ge(s_P, Ug - NB + 1)
                        off = qlo - 128 * (m - 1)
                        sp_ = Sps[Ug % NB][:, off:off + W]
                        q = Qh[0:64, kcols(d, g, qlo, W)]
                        pe.matmul(sp_, Kh[0:64, kcols(d, g, 128 * m, 128)], q, start=True, stop=False)
                        pe.matmul(sp_, identc[:], Bt[:, pi, off:off + W], start=False, stop=True).then_inc(s_S, 1)

                    def PV(ui):
                        pi, d, g, m, nt_, qlo, W = lst[ui]
                        Ug = U0 + ui
                        if pi not in vwaited:
                            vwaited.add(pi)
                            pe.wait_ge(s_v[pi], 16 * NVD[pi] * (TH + 1))
                        pe.wait_ge(s_P, Ug + 1)
                        if Ug >= NB:
                            pe.wait_ge(s_A, Ug - NB + 1)
                        off = qlo - 128 * (m - 1)
                        pe.matmul(Ops[Ug % NB][:, off:off + W], Vd[pi][:, g * nt_ + m, :], P[Ug % NB][:, off:off + W],
                                  start=True, stop=True).then_inc(s_O, 1)

                    n = len(lst)
                    for ui in range(min(NB - 1, n)):
                        QK(ui)
                    for ui in range(n):
                        if ui + NB - 1 < n:
                            QK(ui + NB - 1)
                        PV(ui)

            @block.scalar
            def _(act):
                apend = []
                for TH, (t, h) in enumerate(ths):
                    lst = units[(t, h)]
                    U0 = ustart[(t, h)]
                    for ui in range(len(lst)):
                        pi, d, g, m, nt_, qlo, W = lst[ui]
                        Ug = U0 + ui
                        off = qlo - 128 * (m - 1)
                        act.wait_ge(s_S, Ug + 1)
                        if Ug >= NB:
                            act.wait_ge(s_O, Ug - NB + 1)
                        act.activation(out=P[Ug % NB][:, off:off + W], in_=Sps[Ug % NB][:, off:off + W], func=AF.Exp).then_inc(s_P, 1)
                        if apend and ui >= apend[0][0]:
                            apend.pop(0)[1]()
                    while apend:
                        apend.pop(0)[1]()

                    def mk_act(ci, cw):
                        def piece():
                            act.wait_ge(s_den[ci % 2], 16 * (ci // 2 + 1))
                            act.activation(out=DEN[ci % 2][:, 0:cw], in_=DEN[ci % 2][:, 0:cw], func=AF.Ln).then_inc(s_ln, 1)
                            act.wait_ge(s_ln, ci + 1)
                            if ci >= 1:
                                act.wait_ge(s_F, ci)
                            act.activation(out=RD[:, 0:cw], in_=DEN[ci % 2][:, 0:cw], func=AF.Exp, scale=-1.0).then_inc(s_rd, 1)
                        return piece

                    for k, (ci, c0, cw) in enumerate(chunks[(t, h)]):
                        apend.append((4 * k + 2, mk_act(ci, cw)))
                while apend:
                    apend.pop(0)[1]()

            @block.vector
            def _(v):
                for i in range(3):
                    v.memset(Vd[i][:, :, 64:128], 1.0).then_inc(s_one, 1)
                v.wait_ge(s_c, 48)
                nbt = 0
                zc = {"n": 0}
                dpend = []
                for TH, (t, h) in enumerate(ths):
                    lst = units[(t, h)]
                    U0 = ustart[(t, h)]
                    if TH >= 1:
                        v.wait_ge(s_S, U0)
                    for pi, d in enumerate(PATS):
                        v.scalar_tensor_tensor(out=Bt[:, pi, :], in0=dist[:], scalar=float(SLOPE_B[h] * d), in1=mask[:],
                                               op0=ALU.mult, op1=ALU.add).then_inc(s_bt, 1)
                        nbt += 1
                    ACC = ACC2[TH % 2]
                    if TH < 2:
                        v.memset(ACC[:, 0:NQMAX], 0.0).then_inc(s_z, 1)
                        zc["n"] += 1
                    v.wait_ge(s_z, zc["n"])
                    for ui in range(len(lst)):
                        pi, d, g, m, nt_, qlo, W = lst[ui]
                        Ug = U0 + ui
                        off = qlo - 128 * (m - 1)
                        v.wait_ge(s_O, Ug + 1)
                        if ui >= 1:
                            v.wait_ge(s_A, Ug)
                        dst = ACC[:, kcols(d, g, qlo, W)]
                        v.tensor_tensor(out=dst, in0=Ops[Ug % NB][:, off:off + W], in1=dst, op=ALU.add).then_inc(s_A, 1)
                        if dpend and ui >= dpend[0][0]:
                            dpend.pop(0)[1]()
                    while dpend:
                        dpend.pop(0)[1]()
                    v.wait_ge(s_A, U0 + len(lst))

                    def mk_dve(ci, c0, cw, ACC=ACC):
                        def piece():
                            v.wait_ge(s_rd, ci + 1)
                            if ci >= 2:
                                v.wait_ge(s_ost[ci % 2], 16 * (ci // 2))
                            v.tensor_tensor(out=outb[ci % 2][:, 0:cw], in0=ACC[0:64, c0:c0 + cw], in1=RD[:, 0:cw], op=ALU.mult).then_inc(s_F, 1)
                            v.wait_ge(s_F, ci + 1)
                        return piece

                    nchk = len(chunks[(t, h)])
                    for k, (ci, c0, cw) in enumerate(chunks[(t, h)]):
                        dpend.append((4 * k + 4, mk_dve(ci, c0, cw)))

                    def mk_zero(c0, c1, ACC=ACC):
                        def piece():
                            v.memset(ACC[:, c0:c1], 0.0).then_inc(s_z, 1)
                            zc["n"] += 1
                        return piece

                    for k in range(4):
                        dpend.append((4 * (nchk + k) + 4, mk_zero(k * (NQMAX // 4), (k + 1) * (NQMAX // 4))))
                while dpend:
                    dpend.pop(0)[1]()

            @block.gpsimd
            def _(gp):
                for TH, (t, h) in enumerate(ths):
                    lst = units[(t, h)]
                    U0 = ustart[(t, h)]
                    gp.wait_ge(s_A, U0 + len(lst))
                    ACC = ACC2[TH % 2]
                    chl = chunks[(t, h)]

                    def store(ci, c0, cw):
                        gp.wait_ge(s_F, ci + 1)
                        q0 = QOFF[t] + c0
                        gp.dma_start(out=mixT[512 + h * 64:512 + (h + 1) * 64, q0:q0 + cw], in_=outb[ci % 2][:, 0:cw]).then_inc(s_ost[ci % 2], 16)

                    for k, (ci, c0, cw) in enumerate(chl):
                        if k >= 2:
                            store(*chl[k - 2])
                        if ci >= 2:
                            gp.wait_ge(s_F, ci - 1)
                        gp.dma_start(out=DEN[ci % 2][:, 0:cw], in_=ACC[64:128, c0:c0 + cw]).then_inc(s_den[ci % 2], 16)
                    for k in range(max(0, len(chl) - 2), len(chl)):
                        store(*chl[k])
                for k in range(2):
                    gp.wait_ge(s_ost[k], 16 * ((nch - k + 1) // 2))

    def phase_D():
        TT = 256
        ntile = NQT // TT
        def xrow(n):
            for t in range(NT):
                if n < QOFF[t] + NQS[t]:
                    return t * S + (n - QOFF[t])
            raise AssertionError
        with ExitStack() as es:
            e = es.enter_context
            wo = e(nc.sbuf_tensor("d_wo", [128, 8, DM], BF16))
            wu = e(nc.sbuf_tensor("d_wu", [128, 8, DFF], BF16))
            wd = e(nc.sbuf_tensor("d_wd", [128, 32, DM], BF16))
            xt = [e(nc.sbuf_tensor(f"d_xt{i}", [128, 2, DM], F32)) for i in range(2)]
            mt = [e(nc.sbuf_tensor(f"d_mt{i}", [128, 8, TT], BF16)) for i in range(2)]
            sd = e(nc.sbuf_tensor("d_sd", [128, 5, TT], F32))
            rs = sd
            mixn = e(nc.sbuf_tensor("d_mixn", [128, 8, TT], BF16))
            sqm = mixn
            gm = e(nc.sbuf_tensor("d_gm", [128, 8], F32))
            gmlp = e(nc.sbuf_tensor("d_gmlp", [128, DM], F32))
            gfin = e(nc.sbuf_tensor("d_gfin", [128, DM], F32))
            ssq = e(nc.sbuf_tensor("d_ssq", [128, 4], F32))
            std = e(nc.sbuf_tensor("d_std", [128, 4], F32))
            rstd = e(nc.sbuf_tensor("d_rstd", [128, 4], F32))
            h2 = e(nc.sbuf_tensor("d_h2", [128, 2, DM], BF16))
            junk = h2[:, 0, :]
            h2T = e(nc.sbuf_tensor("d_h2T", [128, 8, TT], BF16))
            rl = [e(nc.sbuf_tensor(f"d_rl{i}", [128, TT], F32)) for i in range(2)]
            aT = [e(nc.sbuf_tensor(f"d_aT{i}", [128, TT], BF16)) for i in range(4)]
            ones = e(nc.sbuf_tensor("d_ones", [128, 128], BF16))
            ident = e(nc.sbuf_tensor("d_ident", [128, 128], BF16))
            yb = [[e(nc.psum_tensor(f"d_yb{j}{hf}", [128, 512], F32)) for hf in range(2)] for j in range(2)]
            ub = [e(nc.psum_tensor(f"d_ub{i}", [128, 512], F32)) for i in range(2)]
            tp = [e(nc.psum_tensor(f"d_tp{i}", [128, 512], BF16)) for i in range(2)]
            S_ = lambda n: e(nc.semaphore("d_" + n))
            s_c = S_("c"); s_w = [S_("w0"), S_("w1")]; s_wc = [S_("wc0"), S_("wc1")]; s_g = S_("g")
            s_xm = [S_("xm0"), S_("xm1")]; s_dsq = S_("dsq"); s_pen = S_("pen"); s_asd = S_("asd"); s_aln = S_("aln")
            s_dmx = S_("dmx"); s_peo = S_("peo"); s_dx1 = S_("dx1"); s_a1 = S_("a1"); s_as2 = S_("as2")
            s_dr2 = S_("dr2"); s_dh2 = S_("dh2"); s_pet = S_("pet"); s_at = S_("at")
            s_peu = S_("peu"); s_ar = S_("ar"); s_da = S_("da"); s_ped = S_("ped")
            s_dx2 = S_("dx2"); s_as3 = S_("as3"); s_dr3 = S_("dr3"); s_dout = S_("dout")
            s_st = [S_("st0"), S_("st1")]
            block = e(nc.Block())

            jobs = []
            for q in range(4):
                jobs.append((w_out[q * 256:(q + 1) * 256, :].rearrange("(c p) n -> p c n", p=128), wo[:, q * 2:(q + 1) * 2, :], True))
            for c in range(8):
                for hf in range(2):
                    jobs.append((w_up[c * 128:(c + 1) * 128, hf * 2048:(hf + 1) * 2048], wu[:, c, hf * 2048:(hf + 1) * 2048], False))
            for q in range(16):
                jobs.append((w_down[q * 256:(q + 1) * 256, :].rearrange("(c p) n -> p c n", p=128), wd[:, q * 2:(q + 1) * 2, :], True))
            NJ = len(jobs)

            def stview(k, shp):
                if shp:
                    return xt[k][:]
                return xt[k][:].rearrange("p c n -> p (c n)")

            @block.sync
            def _(sp):
                sp.dma_start(out=gmlp[:], in_=bcast(g_mlp_t, DM)).then_inc(s_c, 16)
                sp.dma_start(out=gfin[:], in_=bcast(g_fin_t, DM)).then_inc(s_c, 16)
                sp.dma_start(out=ident[:], in_=ident_d).then_inc(s_c, 16)
                sp.dma_start(out=gm[:, 0:4], in_=bass.AP(g_diff_t, 0, [[1, 128], [128, 4]]), allow_slow_non_contiguous=True).then_inc(s_c, 16)
                sp.dma_start(out=gm[:, 4:8], in_=bass.AP(g_dil_t, 0, [[1, 128], [128, 4]]), allow_slow_non_contiguous=True).then_inc(s_c, 16)
                for k, (src, dst, shp) in enumerate(jobs):
                    if k >= 2:
                        sp.wait_ge(s_wc[k % 2], k // 2)
                    sp.dma_start(out=stview(k % 2, shp), in_=src).then_inc(s_w[k % 2], 16)
                sp.wait_ge(s_wc[0], (NJ + 1) // 2)
                sp.wait_ge(s_wc[1], NJ // 2)
                for i in range(ntile):
                    p = i % 2
                    if i >= 2:
                        sp.wait_ge(s_st[p], 16 * (i // 2))
                        sp.wait_ge(s_dmx, i - 1)
                    r = xrow(i * TT)
                    sp.dma_start(out=xt[p][:], in_=x[r:r + TT, :].rearrange("(j p) f -> p j f", p=128)).then_inc(s_xm[p], 16)
                    sp.dma_start(out=mt[p][:], in_=mixT[:, i * TT:(i + 1) * TT].rearrange("(c p) n -> p c n", p=128)).then_inc(s_xm[p], 16)

            @block.scalar
            def _(act):
                for k, (src, dst, shp) in enumerate(jobs):
                    if k % 2 != 0:
                        continue
                    act.wait_ge(s_w[0], 16 * (k // 2 + 1))
                    act.activation(out=dst, in_=stview(0, shp), func=AF.Copy).then_inc(s_wc[0], 1)
                na1 = 0

                def lnexp(i2):
                    act.wait_ge(s_pen, i2 + 1)
                    act.activation(out=sd[:, 0:2, :], in_=yb[0][0][:].rearrange("p (c n) -> p c n", c=2), func=AF.Ln, bias=EPS, scale=1.0 / 128)
                    act.activation(out=sd[:, 2:4, :], in_=yb[0][1][:].rearrange("p (c n) -> p c n", c=2), func=AF.Ln, bias=EPS, scale=1.0 / 128)
                    act.activation(out=sd[:, 4, :], in_=yb[1][0][:, 0:TT], func=AF.Ln, bias=EPS, scale=1.0 / 512).then_inc(s_aln, 1)
                    act.wait_ge(s_aln, i2 + 1)
                    act.activation(out=sd[:], in_=sd[:], func=AF.Exp, scale=-0.5).then_inc(s_asd, 1)

                lnexp(0)
                for i in range(ntile):
                    p = i % 2
                    act.wait_ge(s_dx1, i + 1)
                    for j in range(2):
                        act.activation(out=junk, in_=xt[p][:, j, :], func=AF.Square, accum_out=ssq[:, j:j + 1]).then_inc(s_a1, 1)
                        na1 += 1
                        act.wait_ge(s_a1, na1)
                    act.activation(out=std[:, 0:2], in_=ssq[:, 0:2], func=AF.Sqrt, bias=EPS, scale=1.0 / DM).then_inc(s_as2, 1)
                    for c in range(8):
                        act.wait_ge(s_pet, 8 * i + c + 1)
                        if c == 0 and i >= 1:
                            act.wait_ge(s_peu, 32 * i)
                        act.activation(out=h2T[:, c, :], in_=tp[c % 2][:, 0:TT], func=AF.Copy).then_inc(s_at, 1)
                    if i + 1 < ntile:
                        lnexp(i + 1)
                    for f in range(32):
                        F = 32 * i + f
                        act.wait_ge(s_peu, F + 1)
                        if F >= 2:
                            act.wait_ge(s_da, F - 1)
                        act.activation(out=rl[F % 2][:], in_=ub[F % 2][:, 0:TT], func=AF.Relu).then_inc(s_ar, 1)
                    act.wait_ge(s_dx2, i + 1)
                    for j in range(2):
                        act.activation(out=junk, in_=xt[p][:, j, :], func=AF.Square, accum_out=ssq[:, 2 + j:3 + j]).then_inc(s_a1, 1)
                        na1 += 1
                        act.wait_ge(s_a1, na1)
                    act.activation(out=std[:, 2:4], in_=ssq[:, 2:4], func=AF.Sqrt, bias=EPS, scale=1.0 / DM).then_inc(s_as3, 1)

            @block.vector
            def _(v):
                v.memset(ones[:], 1.0).then_inc(s_g, 1)
                for k, (src, dst, shp) in enumerate(jobs):
                    if k % 2 != 1:
                        continue
                    v.wait_ge(s_w[1], 16 * (k // 2 + 1))
                    v.tensor_copy(out=dst, in_=stview(1, shp)).then_inc(s_wc[1], 1)
                v.wait_ge(s_c, 80)
                v.tensor_scalar(out=gm[:, 0:4], in0=gm[:, 0:4], scalar1=1.0 - LAM_INIT, scalar2=None, op0=ALU.mult).then_inc(s_g, 1)
                v.wait_ge(s_g, 2)
                def sqm_(i2):
                    p2 = i2 % 2
                    v.wait_ge(s_xm[p2], 32 * (i2 // 2 + 1))
                    if i2 >= 1:
                        v.wait_ge(s_peo, i2)
                    v.tensor_tensor(out=sqm[:], in0=mt[p2][:], in1=mt[p2][:], op=ALU.mult).then_inc(s_dsq, 1)

                def mixn_(i2, c):
                    p2 = i2 % 2
                    if c == 0:
                        v.wait_ge(s_asd, i2 + 1)
                    ins = v.scalar_tensor_tensor(out=mixn[:, c, :], in0=mt[p2][:, c, :], scalar=gm[:, c:c + 1],
                                                 in1=rs[:, min(c, 4), :], op0=ALU.mult, op1=ALU.mult)
                    if c == 7:
                        ins.then_inc(s_dmx, 1)

                sqm_(0)
                for c in range(8):
                    mixn_(0, c)
                for i in range(ntile):
                    p = i % 2
                    v.wait_ge(s_peo, i + 1)
                    for j in range(2):
                        for hf in range(2):
                            ins = v.tensor_tensor(out=xt[p][:, j, hf * 512:(hf + 1) * 512], in0=yb[j][hf][:],
                                                  in1=xt[p][:, j, hf * 512:(hf + 1) * 512], op=ALU.add)
                    ins.then_inc(s_dx1, 1)
                    v.wait_ge(s_as2, i + 1)
                    v.reciprocal(out=rstd[:, 0:2], in_=std[:, 0:2]).then_inc(s_dr2, 1)
                    v.wait_ge(s_dr2, i + 1)
                    if i >= 1:
                        v.wait_ge(s_pet, 8 * i)
                    for j in range(2):
                        ins = v.scalar_tensor_tensor(out=h2[:, j, :], in0=xt[p][:, j, :], scalar=rstd[:, j:j + 1],
                                                     in1=gmlp[:], op0=ALU.mult, op1=ALU.mult)
                    ins.then_inc(s_dh2, 1)
                    if i + 1 < ntile:
                        sqm_(i + 1)
                    for f in range(32):
                        F = 32 * i + f
                        v.wait_ge(s_ar, F + 1)
                        if F >= 4:
                            v.wait_ge(s_ped, F - 3)
                        v.tensor_tensor(out=aT[F % 4][:], in0=rl[F % 2][:], in1=rl[F % 2][:], op=ALU.mult).then_inc(s_da, 1)
                        if i + 1 < ntile and 6 <= f < 14:
                            mixn_(i + 1, f - 6)
                    v.wait_ge(s_ped, 32 * (i + 1))
                    for j in range(2):
                        for hf in range(2):
                            ins = v.tensor_tensor(out=xt[p][:, j, hf * 512:(hf + 1) * 512], in0=yb[j][hf][:],
                                                  in1=xt[p][:, j, hf * 512:(hf + 1) * 512], op=ALU.add)
                    ins.then_inc(s_dx2, 1)
                    v.wait_ge(s_as3, i + 1)
                    v.reciprocal(out=rstd[:, 2:4], in_=std[:, 2:4]).then_inc(s_dr3, 1)
                    v.wait_ge(s_dr3, i + 1)
                    for j in range(2):
                        ins = v.scalar_tensor_tensor(out=xt[p][:, j, :], in0=xt[p][:, j, :], scalar=rstd[:, 2 + j:3 + j],
                                                     in1=gfin[:], op0=ALU.mult, op1=ALU.mult)
                    ins.then_inc(s_dout, 1)

            @block.tensor
            def _(pe):
                pe.wait_ge(s_wc[0], (NJ + 1) // 2)
                pe.wait_ge(s_wc[1], NJ // 2)
                pe.wait_ge(s_g, 1)
                pe.wait_ge(s_c, 80)
                def normmm(i2):
                    pe.wait_ge(s_dsq, i2 + 1)
                    for c in range(4):
                        pe.matmul(yb[0][c // 2][:, (c % 2) * TT:(c % 2 + 1) * TT], ones[:], sqm[:, c, :], start=True, stop=True)
                    for c in range(4, 8):
                        ins = pe.matmul(yb[1][0][:, 0:TT], ones[:], sqm[:, c, :], start=(c == 4), stop=(c == 7))
                    ins.then_inc(s_pen, 1)

                normmm(0)
                for i in range(ntile):
                    pe.wait_ge(s_dmx, i + 1)
                    if i >= 1:
                        pe.wait_ge(s_dx2, i)
                    for j in range(2):
                        for hf in range(2):
                            for c in range(8):
                                ins = pe.matmul(yb[j][hf][:], mixn[:, c, j * 128:(j + 1) * 128], wo[:, c, hf * 512:(hf + 1) * 512],
                                                start=(c == 0), stop=(c == 7))
                    ins.then_inc(s_peo, 1)
                    pe.wait_ge(s_dh2, i + 1)
                    for c in range(8):
                        T = 8 * i + c
                        if T >= 2:
                            pe.wait_ge(s_at, T - 1)
                        for j in range(2):
                            ins = pe.transpose(tp[c % 2][:, j * 128:(j + 1) * 128], h2[:, j, c * 128:(c + 1) * 128], ident[:])
                        ins.then_inc(s_pet, 1)
                    pe.wait_ge(s_dx1, i + 1)
                    if i + 1 < ntile:
                        normmm(i + 1)
                    pe.wait_ge(s_at, 8 * (i + 1))

                    def up(f):
                        F = 32 * i + f
                        if F >= 2:
                            pe.wait_ge(s_ar, F - 1)
                        for c in range(8):
                            ins = pe.matmul(ub[F % 2][:, 0:TT], wu[:, c, f * 128:(f + 1) * 128], h2T[:, c, :], start=(c == 0), stop=(c == 7))
                        ins.then_inc(s_peu, 1)

                    def down(f):
                        F = 32 * i + f
                        pe.wait_ge(s_da, F + 1)
                        if f == 0 and i + 1 < ntile:
                            pe.wait_ge(s_asd, i + 2)
                        for j in range(2):
                            for hf in range(2):
                                ins = pe.matmul(yb[j][hf][:], aT[F % 4][:, j * 128:(j + 1) * 128], wd[:, f, hf * 512:(hf + 1) * 512],
                                                start=(f == 0), stop=(f == 31))
                        ins.then_inc(s_ped, 1)

                    up(0)
                    for f in range(32):
                        if f + 1 < 32:
                            up(f + 1)
                        down(f)

            @block.gpsimd
            def _(g):
                for i in range(ntile):
                    g.wait_ge(s_dout, i + 1)
                    g.dma_start(out=y[i * TT:(i + 1) * TT, :].rearrange("(j p) f -> p j f", p=128), in_=xt[i % 2][:]).then_inc(s_st[i % 2], 16)
                for k in range(2):
                    g.wait_ge(s_st[k], 16 * ((ntile - k + 1) // 2))

    if "A" in phases:
        phase_A()
    if "B" in phases:
        phase_B()
    if "C" in phases:
        phase_C()
    if "D" in phases:
        phase_D()
    return nc


def core_inputs(S, NQS, xs, weights, consts):
    m = {"x": np.ascontiguousarray(np.concatenate(xs, axis=0), dtype=np.float32)}
    m.update(weights)
    m.update(consts)
    return m


def prep_weights(g_mix, w_in, lam_qk, g_diff, g_dil, w_out, g_mlp, w_up, w_down, g_final):
    f = lambda a, shp: np.ascontiguousarray(np.asarray(a, dtype=np.float32).reshape(shp))
    return {
        "w_in": f(w_in, (DM, INW)), "w_out": f(w_out, (DM, DM)), "w_up": f(w_up, (DM, DFF)), "w_down": f(w_down, (DFF, DM)),
        "g_mix": f(g_mix, (1, DM)), "g_mlp": f(g_mlp, (1, DM)), "g_final": f(g_final, (1, DM)),
        "g_diff": f(g_diff, (4, 128)), "g_dil": f(g_dil, (1, 512)), "lam_qk": f(lam_qk, (1, 256)),
    }


def kernel(x_prompt, x_sample, g_mix, w_in, lam_qk, g_diff, g_dil, w_out, g_mlp, w_up, w_down, g_final):
    S = SEQ
    NQS = [S, S // 2]
    xp = np.asarray(x_prompt, dtype=np.float32)
    xs = np.asarray(x_sample, dtype=np.float32)
    seqs = [xp[i] for i in range(xp.shape[0])] + [xs[i] for i in range(xs.shape[0])]
    assert len(seqs) == 12
    weights = prep_weights(g_mix, w_in, lam_qk, g_diff, g_dil, w_out, g_mlp, w_up, w_down, g_final)
    consts = host_consts(S)
    in_maps = []
    for c in range(N_CORES):
        s1 = seqs[8 + c // 2]
        if c % 2 == 1:
            s1 = s1[::-1]
        in_maps.append(core_inputs(S, NQS, [seqs[c], s1], weights, consts))
    nc = build_program(S, NQS)
    res = run_bass_kernel_spmd(nc, in_maps, core_ids=list(range(N_CORES)))
    outs = [np.empty((S, DM), np.float32) for _ in range(12)]
    for c in range(N_CORES):
        yc = np.asarray(res.results[c]["y"])
        outs[c][:] = yc[0:S]
        half = yc[S:S + S // 2]
        if c % 2 == 0:
            outs[8 + c // 2][0:S // 2] = half
        else:
            outs[8 + c // 2][S // 2:] = half[::-1]
    y_prompt = np.stack(outs[0:4], axis=0)
    y_sample = np.stack(outs[4:12], axis=0)
    return (y_prompt, y_sample)
```

```python
import numpy as np
import ml_dtypes
from contextlib import ExitStack
import concourse.bass as bass
import concourse.mybir as mybir
from concourse.bass_utils import run_bass_kernel_spmd

F32 = mybir.dt.float32
BF16 = mybir.dt.bfloat16
AF = mybir.ActivationFunctionType
ALU = mybir.AluOpType
AX = mybir.AxisListType

DM = 1024
DFF = 4096
INW = 3072
EPS = 1e-5
SLOPE_A = [2.0 ** (-8.0 * (i + 1) / 4) for i in range(4)]
SLOPE_B = [2.0 ** (-8.0 * (i + 1) / 8) for i in range(8)]
import os
PATS = tuple(int(a) for a in os.environ.get("DBG_PATS", "1,4,16").split(","))
DBG = os.environ.get("DBG", "")
LAM_INIT = 0.2
NEG = -30000.0

SEQ = 8192
N_CORES = 8


def host_consts(S):
    nkt = S // 128
    c = {}
    c["ident"] = np.eye(128, dtype=np.float32).astype(ml_dtypes.bfloat16)
    cb = np.zeros((128, 4 * nkt), np.float32)
    for h in range(4):
        cb[:, h * nkt:(h + 1) * nkt] = -SLOPE_A[h] * 128.0 * np.arange(nkt, dtype=np.float32)[None, :]
    c["cb"] = cb
    k = np.arange(128, dtype=np.float32)[:, None]
    q = np.arange(512, dtype=np.float32)[None, :]
    c["dtl"] = np.concatenate([-np.abs(q - (k + 128.0 * t)) for t in range(4)], axis=1).astype(np.float32)
    u = np.arange(512)
    qaug = np.zeros((4, 2, 3, 512), np.float32)
    kaug = np.zeros((4, 3, S), np.float32)
    for h in range(4):
        s = SLOPE_A[h]
        lo = np.stack([-s * (u % 256), -s * 256.0 * (u // 256), np.ones(512)], 0)
        qaug[h, 0] = lo
        qaug[h, 1] = -lo
        kaug[h] = np.stack([np.ones(S), np.ones(S), s * (np.arange(S) % 128)], 0)
    c["qaug"] = qaug.astype(ml_dtypes.bfloat16)
    c["kaug"] = kaug.astype(ml_dtypes.bfloat16)
    qq = np.arange(384, dtype=np.float32)[None, :]
    ad = np.abs(qq - k - 128.0)
    c["dist"] = (-ad).astype(np.float32)
    c["mask"] = np.where(ad <= 64, 0.0, NEG).astype(np.float32)
    return c


def build_program(S, NQS, phases="ABCD"):
    NT = len(NQS)
    NTOK = NT * S
    NQT = sum(NQS)
    NKT = S // 128
    QOFF = [sum(NQS[:t]) for t in range(NT)]
    nc = bass.Bass("TRN2", target_bir_lowering=False)

    def din(name, shape, dt=F32):
        return nc.dram_tensor(name, list(shape), dt, kind="ExternalInput")

    x_t = din("x", [NTOK, DM]); x = x_t.ap()
    w_in = din("w_in", [DM, INW]).ap()
    w_out = din("w_out", [DM, DM]).ap()
    w_up = din("w_up", [DM, DFF]).ap()
    w_down = din("w_down", [DFF, DM]).ap()
    g_mix_t = din("g_mix", [1, DM])
    g_mlp_t = din("g_mlp", [1, DM])
    g_fin_t = din("g_final", [1, DM])
    g_diff_t = din("g_diff", [4, 128])
    g_dil_t = din("g_dil", [1, 512])
    lam_t = din("lam_qk", [1, 256])
    ident_d = din("ident", [128, 128], BF16).ap()
    cb_d = din("cb", [128, 4 * NKT]).ap()
    dtl_d = din("dtl", [128, 2048]).ap()
    qaug_d = din("qaug", [4, 2, 3, 512], BF16).ap()
    kaug_d = din("kaug", [4, 3, S], BF16).ap()
    dist_d = din("dist", [128, 384]).ap()
    mask_d = din("mask", [128, 384]).ap()
    y = nc.dram_tensor("y", [NQT, DM], F32, kind="ExternalOutput").ap()

    qaT = nc.dram_tensor("qaT", [512, NTOK], BF16).ap()
    kaT = nc.dram_tensor("kaT", [512, NTOK], BF16).ap()
    qbT = nc.dram_tensor("qbT", [512, NTOK], BF16).ap()
    kbT = nc.dram_tensor("kbT", [512, NTOK], BF16).ap()
    va = nc.dram_tensor("va", [NTOK, 512], BF16).ap()
    vb = nc.dram_tensor("vb", [NTOK, 512], BF16).ap()
    mixT = nc.dram_tensor("mixT", [1024, NQT], BF16).ap()

    def bcast(th, n):
        return bass.AP(th, 0, [[0, 128], [1, n]])

    def phase_A():
        ntile = NTOK // 512
        groups = []
        for i4 in range(4):
            groups.append(("fm", i4 * 128, qaT, i4 * 128, 0.125))
        for i4 in range(4):
            groups.append(("fm", 512 + i4 * 128, kaT, i4 * 128, 1.0))
        for i4 in range(4):
            groups.append(("fm", 1536 + i4 * 128, qbT, i4 * 128, 0.125))
        for i4 in range(4):
            groups.append(("fm", 2048 + i4 * 128, kbT, i4 * 128, 1.0))
        for j in range(4):
            groups.append(("tm", 1024, va, j, 1.0))
        for j in range(4):
            groups.append(("tm", 2560, vb, j, 1.0))
        NG = len(groups)
        with ExitStack() as es:
            e = es.enter_context
            wbf = e(nc.sbuf_tensor("a_wbf", [128, 8, INW], BF16))
            wst = [e(nc.sbuf_tensor(f"a_wst{i}", [128, INW], F32)) for i in range(2)]
            xt = [e(nc.sbuf_tensor(f"a_xt{i}", [128, 4, DM], F32)) for i in range(2)]
            junk = e(nc.sbuf_tensor("a_junk", [128, DM], BF16))
            ssq = [e(nc.sbuf_tensor(f"a_ssq{i}", [128, 4], F32)) for i in range(2)]
            std = [e(nc.sbuf_tensor(f"a_std{i}", [128, 4], F32)) for i in range(2)]
            rstd = [e(nc.sbuf_tensor(f"a_rstd{i}", [128, 4], F32)) for i in range(2)]
            hb = e(nc.sbuf_tensor("a_hb", [128, 4, DM], BF16))
            hT = [e(nc.sbuf_tensor(f"a_hT{i}", [128, 8, 512], BF16)) for i in range(2)]
            gbc = e(nc.sbuf_tensor("a_gbc", [128, DM], F32))
            ident = e(nc.sbuf_tensor("a_ident", [128, 128], BF16))
            stg = [e(nc.sbuf_tensor(f"a_stg{i}", [128, 512], BF16)) for i in range(4)]
            ptr = [e(nc.psum_tensor(f"a_ptr{i}", [128, 512], BF16)) for i in range(2)]
            pm = [e(nc.psum_tensor(f"a_pm{i}", [128, 512], F32)) for i in range(4)]
            s_w = [e(nc.semaphore(f"a_w{i}")) for i in range(2)]; s_wc = [e(nc.semaphore(f"a_wc{i}")) for i in range(2)]
            s_c = e(nc.semaphore("a_c"))
            s_x = [e(nc.semaphore(f"a_x{i}")) for i in range(2)]; s_a1 = e(nc.semaphore("a_a1")); s_std = e(nc.semaphore("a_std"))
            s_d1 = e(nc.semaphore("a_d1")); s_h = e(nc.semaphore("a_h"))
            s_trc = e(nc.semaphore("a_trc")); s_tre = e(nc.semaphore("a_tre"))
            s_mm = e(nc.semaphore("a_mm")); s_ev = [e(nc.semaphore(f"a_ev{i}")) for i in range(2)]
            s_st = [e(nc.semaphore(f"a_st{i}")) for i in range(4)]
            block = e(nc.Block())

            @block.sync
            def _(sp):
                sp.dma_start(out=gbc[:], in_=bcast(g_mix_t, DM)).then_inc(s_c, 16)
                sp.dma_start(out=ident[:], in_=ident_d).then_inc(s_c, 16)
                for c in range(8):
                    if c >= 2:
                        sp.wait_ge(s_wc[c % 2], c // 2)
                    sp.dma_start(out=wst[c % 2][:], in_=w_in[c * 128:(c + 1) * 128, :]).then_inc(s_w[c % 2], 16)
                for i in range(ntile):
                    if i >= 2:
                        sp.wait_ge(s_h, i - 1)
                    sp.dma_start(out=xt[i % 2][:], in_=x[i * 512:(i + 1) * 512, :].rearrange("(j p) f -> p j f", p=128)).then_inc(s_x[i % 2], 16)

            @block.scalar
            def _(act):
                for c in range(0, 8, 2):
                    act.wait_ge(s_w[0], 16 * (c // 2 + 1))
                    act.activation(out=wbf[:, c, :], in_=wst[0][:], func=AF.Copy).then_inc(s_wc[0], 1)
                def norm_stats(i2):
                    p2 = i2 % 2
                    act.wait_ge(s_x[p2], 16 * (i2 // 2 + 1))
                    for j in range(4):
                        act.activation(out=junk[:], in_=xt[p2][:, j, :], func=AF.Square,
                                       accum_out=ssq[p2][:, j:j + 1]).then_inc(s_a1, 1)
                        act.wait_ge(s_a1, 4 * i2 + j + 1)
                    act.activation(out=std[p2][:], in_=ssq[p2][:], func=AF.Sqrt, bias=EPS, scale=1.0 / DM).then_inc(s_std, 1)

                norm_stats(0)
                for i in range(ntile):
                    p = i % 2
                    for c in range(8):
                        act.wait_ge(s_trc, 8 * i + c + 1)
                        if i >= 2 and c == 0:
                            act.wait_ge(s_mm, NG * (i - 1))
                        act.activation(out=hT[p][:, c, :], in_=ptr[c % 2][:], func=AF.Copy).then_inc(s_tre, 1)
                    if i + 1 < ntile:
                        norm_stats(i + 1)
                    for n in range(NG):
                        G = NG * i + n
                        if G % 2 != 0:
                            continue
                        kind, col0, dst, idx, scale = groups[n]
                        act.wait_ge(s_mm, G + 1)
                        if G >= 4:
                            act.wait_ge(s_st[G % 4], 16 * (G // 4))
                        act.activation(out=stg[G % 4][:], in_=pm[G % 4][:], func=AF.Copy, scale=scale).then_inc(s_ev[0], 1)

            @block.vector
            def _(v):
                for c in range(1, 8, 2):
                    v.wait_ge(s_w[1], 16 * (c // 2 + 1))
                    v.tensor_copy(out=wbf[:, c, :], in_=wst[1][:]).then_inc(s_wc[1], 1)
                v.wait_ge(s_c, 32)
                def hstage(i2):
                    p2 = i2 % 2
                    v.wait_ge(s_std, i2 + 1)
                    v.reciprocal(out=rstd[p2][:], in_=std[p2][:]).then_inc(s_d1, 1)
                    v.wait_ge(s_d1, i2 + 1)
                    if i2 >= 1:
                        v.wait_ge(s_trc, 8 * i2)
                    for j in range(4):
                        ins = v.scalar_tensor_tensor(out=hb[:, j, :], in0=xt[p2][:, j, :], scalar=rstd[p2][:, j:j + 1],
                                                     in1=gbc[:], op0=ALU.mult, op1=ALU.mult)
                    ins.then_inc(s_h, 1)

                hstage(0)
                for i in range(ntile):
                    p = i % 2
                    if i + 1 < ntile:
                        hstage(i + 1)
                    for n in range(NG):
                        G = NG * i + n
                        if G % 2 != 1:
                            continue
                        kind, col0, dst, idx, scale = groups[n]
                        v.wait_ge(s_mm, G + 1)
                        if G >= 4:
                            v.wait_ge(s_st[G % 4], 16 * (G // 4))
                        if scale != 1.0:
                            v.tensor_scalar(out=stg[G % 4][:], in0=pm[G % 4][:], scalar1=scale, scalar2=None,
                                            op0=ALU.mult).then_inc(s_ev[1], 1)
                        else:
                            v.tensor_copy(out=stg[G % 4][:], in_=pm[G % 4][:]).then_inc(s_ev[1], 1)

            @block.tensor
            def _(pe):
                pe.wait_ge(s_wc[0], 4)
                pe.wait_ge(s_wc[1], 4)
                pe.wait_ge(s_c, 32)
                for i in range(ntile):
                    p = i % 2
                    pe.wait_ge(s_h, i + 1)
                    for c in range(8):
                        T = 8 * i + c
                        if T >= 2:
                            pe.wait_ge(s_tre, T - 1)
                        for j in range(4):
                            ins = pe.transpose(ptr[c % 2][:, j * 128:(j + 1) * 128], hb[:, j, c * 128:(c + 1) * 128], ident[:])
                        ins.then_inc(s_trc, 1)
                    pe.wait_ge(s_tre, 8 * (i + 1))
                    for n in range(NG):
                        G = NG * i + n
                        kind, col0, dst, idx, scale = groups[n]
                        if G >= 4:
                            Gp = G - 4
                            pe.wait_ge(s_ev[Gp % 2], Gp // 2 + 1)
                        for c in range(8):
                            if kind == "fm":
                                ins = pe.matmul(pm[G % 4][:], wbf[:, c, col0:col0 + 128], hT[p][:, c, :],
                                                start=(c == 0), stop=(c == 7))
                            else:
                                ins = pe.matmul(pm[G % 4][:], hT[p][:, c, idx * 128:(idx + 1) * 128],
                                                wbf[:, c, col0:col0 + 512], start=(c == 0), stop=(c == 7))
                        ins.then_inc(s_mm, 1)

            @block.gpsimd
            def _(g):
                for i in range(ntile):
                    for n in range(NG):
                        G = NG * i + n
                        kind, col0, dst, idx, scale = groups[n]
                        g.wait_ge(s_ev[G % 2], G // 2 + 1)
                        if kind == "fm":
                            o = dst[idx:idx + 128, i * 512:(i + 1) * 512]
                        else:
                            r0 = i * 512 + idx * 128
                            o = dst[r0:r0 + 128, :]
                        g.dma_start(out=o, in_=stg[G % 4][:]).then_inc(s_st[G % 4], 16)
                tot = NG * ntile
                for k in range(4):
                    g.wait_ge(s_st[k], 16 * ((tot - k + 3) // 4))

    def phase_B():
        ths = [(t, h) for t in range(NT) for h in range(4)]
        blocks = []
        for t, h in ths:
            for qb in range(NQS[t] // 512):
                blocks.append((t, h, qb))
        first_of_th = {}
        for B, (t, h, qb) in enumerate(blocks):
            first_of_th.setdefault((t, h), B)
        NB = len(blocks)
        NKV = 2 + 2 + 4 + 8

        def kind_of(qb, kt):
            if kt < 4 * qb:
                return "lo"
            if kt > 4 * qb + 3:
                return "up"
            return "dg"

        def ndiag_before(B):
            return 4 * B

        with ExitStack() as es:
            e = es.enter_context
            K2 = [[e(nc.sbuf_tensor(f"b_K{p}{m}", [67, S], BF16)) for m in range(2)] for p in range(2)]
            V2 = [e(nc.sbuf_tensor(f"b_V{p}", [128, NKT, 128], BF16)) for p in range(2)]
            Qlo = [[e(nc.sbuf_tensor(f"b_Qlo{p}{m}", [67, 512], BF16)) for m in range(2)] for p in range(2)]
            Qup = [[e(nc.sbuf_tensor(f"b_Qup{p}{m}", [67, 512], BF16)) for m in range(2)] for p in range(2)]
            NP = 3
            P = [e(nc.sbuf_tensor(f"b_P{p}", [128, 1024], BF16)) for p in range(NP)]
            Sb = [e(nc.sbuf_tensor(f"b_Sb{p}", [128, 1024], F32)) for p in range(2)]
            Lacc = e(nc.sbuf_tensor("b_Lacc", [128, 512], F32))
            Lhi = e(nc.sbuf_tensor("b_Lhi", [128, 512], BF16))
            Llo = e(nc.sbuf_tensor("b_Llo", [128, 512], BF16))
            dtl = e(nc.sbuf_tensor("b_dtl", [128, 2048], F32))
            cb = e(nc.sbuf_tensor("b_cb", [128, 4 * NKT], F32))
            ones = e(nc.sbuf_tensor("b_ones", [128, 128], BF16))
            lq = e(nc.sbuf_tensor("b_lq", [128, 256], F32))
            ltmp = e(nc.sbuf_tensor("b_ltmp", [128, 128], F32))
            ldot = e(nc.sbuf_tensor("b_ldot", [128, 2], F32))
            lexp = e(nc.sbuf_tensor("b_lexp", [128, 2], F32))
            lamneg = e(nc.sbuf_tensor("b_lamneg", [128, 1], F32))
            r0 = e(nc.sbuf_tensor("b_r0", [128, 512], F32)); r1 = e(nc.sbuf_tensor("b_r1", [128, 512], F32))
            t0 = e(nc.sbuf_tensor("b_t0", [128, 512], F32)); t1 = e(nc.sbuf_tensor("b_t1", [128, 512], F32))
            ob = [e(nc.sbuf_tensor(f"b_ob{i}", [128, 512], BF16)) for i in range(2)]
            Sps = [e(nc.psum_tensor(f"b_S{p}", [128, 1024], F32)) for p in range(2)]
            O = [e(nc.psum_tensor(f"b_O{m}", [128, 512], F32)) for m in range(2)]
            L = [e(nc.psum_tensor(f"b_L{m}", [128, 512], F32)) for m in range(2)]
            s_c = e(nc.semaphore("b_c")); s_kv = [e(nc.semaphore(f"b_kv{i}")) for i in range(2)]; s_qa = e(nc.semaphore("b_qa")); s_q = [e(nc.semaphore(f"b_q{i}")) for i in range(2)]
            s_S = e(nc.semaphore("b_S")); s_P = e(nc.semaphore("b_P")); s_V = e(nc.semaphore("b_V"))
            s_Bd = e(nc.semaphore("b_Bd")); s_E = e(nc.semaphore("b_E")); s_Eo = e(nc.semaphore("b_Eo"))
            s_ds = e(nc.semaphore("b_ds")); s_lam = e(nc.semaphore("b_lam")); s_la = e(nc.semaphore("b_la"))
            s_ost = [e(nc.semaphore(f"b_ost{i}")) for i in range(2)]
            s_L = e(nc.semaphore("b_L")); s_hl = e(nc.semaphore("b_hl")); s_Lb = e(nc.semaphore("b_Lb"))
            s_E0 = e(nc.semaphore("b_E0"))
            block = e(nc.Block())

            def msl(m):
                return slice(m * 512, (m + 1) * 512)

            @block.sync
            def _(sp):
                sp.dma_start(out=dtl[:], in_=dtl_d).then_inc(s_c, 16)
                sp.dma_start(out=cb[:], in_=cb_d).then_inc(s_c, 16)
                sp.dma_start(out=lq[:], in_=bcast(lam_t, 256)).then_inc(s_c, 16)
                def load_kv(TH2):
                    t2, h2_ = ths[TH2]
                    pk = TH2 % 2
                    for m in range(2):
                        sp.dma_start(out=K2[pk][m][0:64, :], in_=kaT[h2_ * 128 + m * 64:h2_ * 128 + m * 64 + 64, t2 * S:(t2 + 1) * S]).then_inc(s_kv[pk], 16)
                        sp.dma_start(out=K2[pk][m][64:67, :], in_=kaug_d[h2_]).then_inc(s_kv[pk], 16)
                    nq4 = NKT // 4
                    for q4 in range(4):
                        src = va[t2 * S + q4 * nq4 * 128:t2 * S + (q4 + 1) * nq4 * 128, h2_ * 128:(h2_ + 1) * 128]
                        sp.dma_start(out=V2[pk][:, q4 * nq4:(q4 + 1) * nq4, :], in_=src.rearrange("(k p) e -> p k e", p=128)).then_inc(s_kv[pk], 16)

                load_kv(0)
                for B, (t, h, qb) in enumerate(blocks):
                    if first_of_th[(t, h)] == B:
                        TH = ths.index((t, h))
                        if B > 0:
                            sp.wait_ge(s_V, B * NKT)
                        for p in range(2):
                            for m in range(2):
                                sp.dma_start(out=Qlo[p][m][64:67, :], in_=qaug_d[h, 0]).then_inc(s_qa, 16)
                                sp.dma_start(out=Qup[p][m][64:67, :], in_=qaug_d[h, 1]).then_inc(s_qa, 16)
                        if TH + 1 < len(ths):
                            load_kv(TH + 1)
                    if B >= 2:
                        sp.wait_ge(s_S, (B - 1) * NKT)
                    for m in range(2):
                        src = qaT[h * 128 + m * 64:h * 128 + m * 64 + 64, t * S + qb * 512:t * S + (qb + 1) * 512]
                        sp.dma_start(out=Qlo[B % 2][m][0:64, :], in_=src).then_inc(s_q[B % 2], 16)
                        sp.dma_start(out=Qup[B % 2][m][0:64, :], in_=src).then_inc(s_q[B % 2], 16)

            @block.tensor
            def _(pe):
                pe.wait_ge(s_lam, 1)
                for B, (t, h, qb) in enumerate(blocks):
                    TH = ths.index((t, h))
                    K = K2[TH % 2]
                    V = V2[TH % 2]
                    if first_of_th[(t, h)] == B:
                        pe.wait_ge(s_kv[TH % 2], 16 * 8 * (TH // 2 + 1))
                        pe.wait_ge(s_qa, 16 * 8 * (TH + 1))
                    pe.wait_ge(s_q[B % 2], 64 * (B // 2 + 1))
                    g0 = B * NKT

                    def QK(kt):
                        g = g0 + kt
                        par = g % 2
                        if g >= 2:
                            pe.wait_ge(s_P, g - 1)
                        kd = kind_of(qb, kt)
                        for m in range(2):
                            if kd == "dg":
                                ins = pe.matmul(Sps[par][:, msl(m)], K[m][0:64, kt * 128:(kt + 1) * 128], Qlo[B % 2][m][0:64, :],
                                                start=True, stop=True)
                            else:
                                Q = Qlo if kd == "lo" else Qup
                                ins = pe.matmul(Sps[par][:, msl(m)], K[m][0:67, kt * 128:(kt + 1) * 128], Q[B % 2][m][0:67, :],
                                                start=True, stop=True)
                        ins.then_inc(s_S, 1)

                    def PV(kt):
                        g = g0 + kt
                        pe.wait_ge(s_P, g + 1)
                        if kt == 0 and B >= 1:
                            pe.wait_ge(s_E, B)
                        for m in range(2):
                            pe.matmul(O[m][:], V[:, kt, :], P[g % NP][:, msl(m)], start=(kt == 0), stop=(kt == NKT - 1))
                        ins = pe.matmul(L[1][:], ones[:], P[g % NP][:, msl(1)], start=(kt == 0), stop=(kt == NKT - 1))
                        ins.then_inc(s_V, 1)

                    def Lbcast(Bp):
                        pe.wait_ge(s_hl, Bp + 1)
                        if Bp >= 1:
                            pe.wait_ge(s_E0, Bp)
                        pe.matmul(L[0][:], ones[:], Lhi[:], start=True, stop=False)
                        pe.matmul(L[0][:], ones[:], Llo[:], start=False, stop=True).then_inc(s_Lb, 1)

                    QK(0)
                    if NKT > 1:
                        QK(1)
                    for kt in range(NKT):
                        PV(kt)
                        if kt + 2 < NKT:
                            QK(kt + 2)
                        if kt == 1 and B >= 1:
                            Lbcast(B - 1)
                    if B == NB - 1:
                        Lbcast(B)

            @block.scalar
            def _(act):
                act.wait_ge(s_lam, 2)
                act.activation(out=lexp[:], in_=ldot[:], func=AF.Exp).then_inc(s_la, 1)
                for B, (t, h, qb) in enumerate(blocks):
                    g0 = B * NKT
                    for kt in range(NKT):
                        g = g0 + kt
                        par = g % 2
                        kd = kind_of(qb, kt)
                        if kd == "dg":
                            act.wait_ge(s_Bd, ndiag_before(B) + (kt - 4 * qb) + 1)
                        else:
                            act.wait_ge(s_S, g + 1)
                        if g >= NP:
                            act.wait_ge(s_V, g - NP + 1)
                            act.wait_ge(s_L, g - NP + 1)
                        if kd == "dg":
                            ins = act.activation(out=P[g % NP][:], in_=Sb[par][:], func=AF.Exp)
                        else:
                            n = abs(kt - 4 * qb)
                            ins = act.activation(out=P[g % NP][:], in_=Sps[par][:], func=AF.Exp,
                                                 bias=cb[:, h * NKT + n:h * NKT + n + 1], scale=1.0)
                        ins.then_inc(s_P, 1)

            @block.vector
            def _(v):
                v.memset(ones[:], 1.0).then_inc(s_lam, 1)
                v.wait_ge(s_c, 48)
                v.tensor_tensor(out=ltmp[:, 0:64], in0=lq[:, 0:64], in1=lq[:, 64:128], op=ALU.mult)
                v.tensor_tensor(out=ltmp[:, 64:128], in0=lq[:, 128:192], in1=lq[:, 192:256], op=ALU.mult).then_inc(s_ds, 1)
                v.wait_ge(s_ds, 1)
                v.reduce_sum(out=ldot[:, 0:1], in_=ltmp[:, 0:64], axis=AX.X)
                v.reduce_sum(out=ldot[:, 1:2], in_=ltmp[:, 64:128], axis=AX.X).then_inc(s_lam, 1)
                v.wait_ge(s_la, 1)
                v.tensor_tensor(out=lamneg[:], in0=lexp[:, 1:2], in1=lexp[:, 0:1], op=ALU.subtract).then_inc(s_ds, 1)
                v.wait_ge(s_ds, 2)
                v.tensor_scalar(out=lamneg[:], in0=lamneg[:], scalar1=-LAM_INIT, scalar2=None, op0=ALU.add).then_inc(s_ds, 1)
                v.wait_ge(s_ds, 3)
                st_ = {"nds": 3}
                pending = []
                for B, (t, h, qb) in enumerate(blocks):
                    g0 = B * NKT

                    def biasadd(kt):
                        tt = kt - 4 * qb
                        g = g0 + kt
                        par = g % 2
                        v.wait_ge(s_S, g + 1)
                        if g >= 2:
                            v.wait_ge(s_P, g - 1)
                        for m in range(2):
                            ins = v.scalar_tensor_tensor(out=Sb[par][:, msl(m)], in0=dtl[:, tt * 512:(tt + 1) * 512],
                                                         scalar=float(SLOPE_A[h]), in1=Sps[par][:, msl(m)],
                                                         op0=ALU.mult, op1=ALU.add)
                        ins.then_inc(s_Bd, 1)

                    if kind_of(qb, 0) == "dg":
                        biasadd(0)
                    for kt in range(NKT):
                        g = g0 + kt
                        if kt + 1 < NKT and kind_of(qb, kt + 1) == "dg":
                            biasadd(kt + 1)
                        v.wait_ge(s_P, g + 1)
                        if kt == 0:
                            v.tensor_copy(out=Lacc[:], in_=P[g % NP][:, 0:512]).then_inc(s_L, 1)
                        else:
                            v.wait_ge(s_L, g)
                            v.tensor_tensor(out=Lacc[:], in0=P[g % NP][:, 0:512], in1=Lacc[:], op=ALU.add).then_inc(s_L, 1)
                        if pending and kt >= 4:
                            pending.pop(0)()
                    while pending:
                        pending.pop(0)()
                    v.wait_ge(s_V, g0 + NKT)
                    v.tensor_copy(out=r1[:], in_=L[1][:])
                    v.tensor_copy(out=t0[:], in_=O[0][:])
                    v.tensor_copy(out=t1[:], in_=O[1][:]).then_inc(s_E, 1)
                    v.wait_ge(s_E, B + 1)
                    v.wait_ge(s_L, g0 + NKT)
                    if B >= 1:
                        v.wait_ge(s_Lb, B)
                    v.tensor_copy(out=Lhi[:], in_=Lacc[:]).then_inc(s_ds, 1)
                    st_["nds"] += 1
                    v.wait_ge(s_ds, st_["nds"])
                    v.tensor_tensor(out=Llo[:], in0=Lacc[:], in1=Lhi[:], op=ALU.subtract).then_inc(s_hl, 1)
                    v.wait_ge(s_hl, B + 1)

                    def mk_tail(B=B):
                        def T1():
                            v.wait_ge(s_Lb, B + 1)
                            v.tensor_copy(out=r0[:], in_=L[0][:]).then_inc(s_E0, 1)
                        def T2():
                            v.wait_ge(s_E0, B + 1)
                            v.reciprocal(out=r0[:, 0:256], in_=r0[:, 0:256])
                        def T3():
                            v.reciprocal(out=r0[:, 256:512], in_=r0[:, 256:512])
                        def T4():
                            v.reciprocal(out=r1[:, 0:256], in_=r1[:, 0:256])
                        def T5():
                            v.reciprocal(out=r1[:, 256:512], in_=r1[:, 256:512]).then_inc(s_ds, 1)
                            st_["nds"] += 1
                        def T6():
                            v.wait_ge(s_ds, st_["nds"])
                            v.tensor_tensor(out=t0[:], in0=t0[:], in1=r0[:], op=ALU.mult)
                        def T7():
                            v.tensor_tensor(out=t1[:], in0=t1[:], in1=r1[:], op=ALU.mult).then_inc(s_ds, 1)
                            st_["nds"] += 1
                        def T8():
                            v.wait_ge(s_ds, st_["nds"])
                            if B >= 2:
                                v.wait_ge(s_ost[B % 2], 16 * (B // 2))
                            v.scalar_tensor_tensor(out=ob[B % 2][:], in0=t1[:], scalar=lamneg[:, 0:1], in1=t0[:],
                                                   op0=ALU.mult, op1=ALU.add).then_inc(s_Eo, 1)
                            v.wait_ge(s_Eo, B + 1)
                        return [T1, T2, T3, T4, T5, T6, T7, T8]

                    pending.extend(mk_tail())
                while pending:
                    pending.pop(0)()

            @block.gpsimd
            def _(g):
                for B, (t, h, qb) in enumerate(blocks):
                    g.wait_ge(s_Eo, B + 1)
                    c0 = QOFF[t] + qb * 512
                    g.dma_start(out=mixT[h * 128:(h + 1) * 128, c0:c0 + 512], in_=ob[B % 2][:]).then_inc(s_ost[B % 2], 16)
                for k in range(2):
                    g.wait_ge(s_ost[k], 16 * ((NB - k + 1) // 2))

    def phase_C():
        ths = [(t, h) for t in range(NT) for h in range(8)]
        units = {}
        ustart = {}
        U = 0
        for t, h in ths:
            lst = []
            for pi, d in enumerate(PATS):
                nt_ = S // d // 128
                nqb = NQS[t] // d // 128
                for g in range(d):
                    for m in range(min(nt_, nqb + 1)):
                        b0 = max(0, m - 1)
                        b1 = min(nqb - 1, m + 1)
                        lst.append((pi, d, g, m, nt_, 128 * b0, 128 * (b1 - b0 + 1)))
            units[(t, h)] = lst
            ustart[(t, h)] = U
            U += len(lst)
        NVD = [4 if d == 1 else d for d in PATS]
        pat_end = {}
        for th_ in ths:
            pe_ = {}
            for ui, (pi, d, g, m, nt_, qlo, W) in enumerate(units[th_]):
                pe_[pi] = ustart[th_] + ui + 1
            pat_end[th_] = pe_
        NQMAX = max(NQS)
        CH = 1024
        chunks = {}
        nch = 0
        for t, h in ths:
            l = []
            for c0 in range(0, NQS[t], CH):
                l.append((nch, c0, min(CH, NQS[t] - c0)))
                nch += 1
            chunks[(t, h)] = l

        with ExitStack() as es:
            e = es.enter_context
            Kh2 = [e(nc.sbuf_tensor(f"c_K{i}", [64, S], BF16)) for i in range(2)]
            Qh2 = [e(nc.sbuf_tensor(f"c_Q{i}", [64, S], BF16)) for i in range(2)]
            Vd = [e(nc.sbuf_tensor(f"c_V{i}", [128, NKT, 128], BF16)) for i in range(3)]
            ACC2 = [e(nc.sbuf_tensor(f"c_ACC{i}", [128, NQMAX], F32)) for i in range(2)]
            DEN = [e(nc.sbuf_tensor(f"c_DEN{i}", [64, CH], F32)) for i in range(2)]
            RD = e(nc.sbuf_tensor("c_RD", [64, CH], F32))
            outb = [e(nc.sbuf_tensor(f"c_outb{i}", [64, CH], BF16)) for i in range(2)]
            Bt = e(nc.sbuf_tensor("c_Bt", [128, 3, 384], BF16))
            identc = e(nc.sbuf_tensor("c_ident", [128, 128], BF16))
            dist = e(nc.sbuf_tensor("c_dist", [128, 384], F32))
            mask = e(nc.sbuf_tensor("c_mask", [128, 384], F32))
            NB = 3
            P = [e(nc.sbuf_tensor(f"c_P{i}", [128, 384], BF16)) for i in range(NB)]
            Sps = [e(nc.psum_tensor(f"c_S{i}", [128, 512], F32)) for i in range(NB)]
            Ops = [e(nc.psum_tensor(f"c_O{i}", [128, 512], F32)) for i in range(NB)]
            s_c = e(nc.semaphore("c_c"))
            s_kq = [e(nc.semaphore(f"c_kq{i}")) for i in range(2)]
            s_v = [e(nc.semaphore(f"c_v{i}")) for i in range(3)]
            s_S = e(nc.semaphore("c_S")); s_B = e(nc.semaphore("c_B")); s_P = e(nc.semaphore("c_P"))
            s_O = e(nc.semaphore("c_O")); s_A = e(nc.semaphore("c_A")); s_bt = e(nc.semaphore("c_bt"))
            s_one = e(nc.semaphore("c_one")); s_z = e(nc.semaphore("c_z")); s_zp = e(nc.semaphore("c_zp"))
            s_den = [e(nc.semaphore(f"c_den{i}")) for i in range(2)]
            s_rd = e(nc.semaphore("c_rd")); s_F = e(nc.semaphore("c_F")); s_ln = e(nc.semaphore("c_ln"))
            s_ost = [e(nc.semaphore(f"c_ost{i}")) for i in range(2)]
            block = e(nc.Block())

            def qcols(d, g, b):
                st = 128 * b * d + g
                return slice(st, st + 127 * d + 1, d)

            def kcols(d, g, key0, n):
                st = key0 * d + g
                return slice(st, st + (n - 1) * d + 1, d)

            @block.sync
            def _(sp):
                sp.dma_start(out=dist[:], in_=dist_d).then_inc(s_c, 16)
                sp.dma_start(out=mask[:], in_=mask_d).then_inc(s_c, 16)
                sp.dma_start(out=identc[:], in_=ident_d).then_inc(s_c, 16)
                sp.wait_ge(s_one, 3)
                for TH, (t, h) in enumerate(ths):
                    if TH >= 2:
                        sp.wait_ge(s_S, ustart[ths[TH - 1]])
                    sp.dma_start(out=Kh2[TH % 2][:], in_=kbT[h * 64:(h + 1) * 64, t * S:(t + 1) * S]).then_inc(s_kq[TH % 2], 16)
                    sp.dma_start(out=Qh2[TH % 2][:, 0:NQS[t]], in_=qbT[h * 64:(h + 1) * 64, t * S:t * S + NQS[t]]).then_inc(s_kq[TH % 2], 16)
                    for pi, d in enumerate(PATS):
                        nt_ = S // d // 128
                        if TH >= 1:
                            sp.wait_ge(s_O, pat_end[ths[TH - 1]][pi])
                        if d == 1:
                            nq4 = nt_ // 4
                            for q4 in range(4):
                                src = vb[t * S + q4 * nq4 * 128:t * S + (q4 + 1) * nq4 * 128, h * 64:(h + 1) * 64]
                                sp.dma_start(out=Vd[pi][:, q4 * nq4:(q4 + 1) * nq4, 0:64],
                                             in_=src.rearrange("(k p) e -> p k e", p=128)).then_inc(s_v[pi], 16)
                        else:
                            for g in range(d):
                                src = vb[t * S:(t + 1) * S, h * 64:(h + 1) * 64].rearrange("(k p d) e -> d p k e", p=128, d=d)[g]
                                sp.dma_start(out=Vd[pi][:, g * nt_:(g + 1) * nt_, 0:64], in_=src).then_inc(s_v[pi], 16)

            @block.tensor
            def _(pe):
                for TH, (t, h) in enumerate(ths):
                    pe.wait_ge(s_kq[TH % 2], 32 * (TH // 2 + 1))
                    pe.wait_ge(s_bt, 3 * (TH + 1))
                    if TH == 0:
                        pe.wait_ge(s_c, 48)
                    lst = units[(t, h)]
                    U0 = ustart[(t, h)]
                    Kh = Kh2[TH % 2]
                    Qh = Qh2[TH % 2]
                    vwaited = set()

                    def QK(ui):
                        pi, d, g, m, nt_, qlo, W = lst[ui]
                        Ug = U0 + ui
                        if Ug >= NB:
                            pe.wait_ge(s_P, Ug - NB + 1)
                        off = qlo - 128 * (m - 1)
                        sp_ = Sps[Ug % NB][:, off:off + W]
                        q = Qh[0:64, kcols(d, g, qlo, W)]
                        pe.matmul(sp_, Kh[0:64, kcols(d, g, 128 * m, 128)], q, start=True, stop=False)
                        pe.matmul(sp_, identc[:], Bt[:, pi, off:off + W], start=False, stop=True).then_inc(s_S, 1)

                    def PV(ui):
                        pi, d, g, m, nt_, qlo, W = lst[ui]
                        Ug = U0 + ui
                        if pi not in vwaited:
                            vwaited.add(pi)
                            pe.wait_ge(s_v[pi], 16 * NVD[pi] * (TH + 1))
                        pe.wait_ge(s_P, Ug + 1)
                        if Ug >= NB:
                            pe.wait_ge(s_A, Ug - NB + 1)
                        off = qlo - 128 * (m - 1)
                        pe.matmul(Ops[Ug % NB][:, off:off + W], Vd[pi][:, g * nt_ + m, :], P[Ug % NB][:, off:off + W],
                                  start=True, stop=True).then_inc(s_O, 1)

                    n = len(lst)
                    for ui in range(min(NB - 1, n)):
                        QK(ui)
                    for ui in range(n):
                        if ui + NB - 1 < n:
                            QK(ui + NB - 1)
                        PV(ui)

            @block.scalar
            def _(act):
                apend = []
                for TH, (t, h) in enumerate(ths):
                    lst = units[(t, h)]
                    U0 = ustart[(t, h)]
                    for ui in range(len(lst)):
                        pi, d, g, m, nt_, qlo, W = lst[ui]
                        Ug = U0 + ui
                        off = qlo - 128 * (m - 1)
                        act.wait_ge(s_S, Ug + 1)
                        if Ug >= NB:
                            act.wait_ge(s_O, Ug - NB + 1)
                        act.activation(out=P[Ug % NB][:, off:off + W], in_=Sps[Ug % NB][:, off:off + W], func=AF.Exp).then_inc(s_P, 1)
                        if apend and ui >= apend[0][0]:
                            apend.pop(0)[1]()
                    while apend:
                        apend.pop(0)[1]()

                    def mk_act(ci, cw):
                        def piece():
                            act.wait_ge(s_den[ci % 2], 16 * (ci // 2 + 1))
                            act.activation(out=DEN[ci % 2][:, 0:cw], in_=DEN[ci % 2][:, 0:cw], func=AF.Ln).then_inc(s_ln, 1)
                            act.wait_ge(s_ln, ci + 1)
                            if ci >= 1:
                                act.wait_ge(s_F, ci)
                            act.activation(out=RD[:, 0:cw], in_=DEN[ci % 2][:, 0:cw], func=AF.Exp, scale=-1.0).then_inc(s_rd, 1)
                        return piece

                    for k, (ci, c0, cw) in enumerate(chunks[(t, h)]):
                        apend.append((4 * k + 2, mk_act(ci, cw)))
                while apend:
                    apend.pop(0)[1]()

            @block.vector
            def _(v):
                for i in range(3):
                    v.memset(Vd[i][:, :, 64:128], 1.0).then_inc(s_one, 1)
                v.wait_ge(s_c, 48)
                nbt = 0
                nz = 0
                dpend = []
                for TH, (t, h) in enumerate(ths):
                    lst = units[(t, h)]
                    U0 = ustart[(t, h)]
                    if TH >= 1:
                        v.wait_ge(s_S, U0)
                    for pi, d in enumerate(PATS):
                        v.scalar_tensor_tensor(out=Bt[:, pi, :], in0=dist[:], scalar=float(SLOPE_B[h] * d), in1=mask[:],
                                               op0=ALU.mult, op1=ALU.add).then_inc(s_bt, 1)
                        nbt += 1
                    ACC = ACC2[TH % 2]
                    if TH < 2:
                        v.memset(ACC[:, 0:NQMAX], 0.0).then_inc(s_z, 1)
                        nz += 1
                        v.wait_ge(s_z, nz)
                    else:
                        v.wait_ge(s_zp, TH - 1)
                    for ui in range(len(lst)):
                        pi, d, g, m, nt_, qlo, W = lst[ui]
                        Ug = U0 + ui
                        off = qlo - 128 * (m - 1)
                        v.wait_ge(s_O, Ug + 1)
                        if ui >= 1:
                            v.wait_ge(s_A, Ug)
                        dst = ACC[:, kcols(d, g, qlo, W)]
                        v.tensor_tensor(out=dst, in0=Ops[Ug % NB][:, off:off + W], in1=dst, op=ALU.add).then_inc(s_A, 1)
                        if dpend and ui >= dpend[0][0]:
                            dpend.pop(0)[1]()
                    while dpend:
                        dpend.pop(0)[1]()
                    v.wait_ge(s_A, U0 + len(lst))

                    def mk_dve(ci, c0, cw, ACC=ACC):
                        def piece():
                            v.wait_ge(s_rd, ci + 1)
                            if ci >= 2:
                                v.wait_ge(s_ost[ci % 2], 16 * (ci // 2))
                            v.tensor_tensor(out=outb[ci % 2][:, 0:cw], in0=ACC[0:64, c0:c0 + cw], in1=RD[:, 0:cw], op=ALU.mult).then_inc(s_F, 1)
                            v.wait_ge(s_F, ci + 1)
                        return piece

                    for k, (ci, c0, cw) in enumerate(chunks[(t, h)]):
                        dpend.append((4 * k + 4, mk_dve(ci, c0, cw)))
                while dpend:
                    dpend.pop(0)[1]()

            @block.gpsimd
            def _(gp):
                for TH, (t, h) in enumerate(ths):
                    lst = units[(t, h)]
                    U0 = ustart[(t, h)]
                    gp.wait_ge(s_A, U0 + len(lst))
                    ACC = ACC2[TH % 2]
                    chl = chunks[(t, h)]

                    def store(ci, c0, cw):
                        gp.wait_ge(s_F, ci + 1)
                        q0 = QOFF[t] + c0
                        gp.dma_start(out=mixT[512 + h * 64:512 + (h + 1) * 64, q0:q0 + cw], in_=outb[ci % 2][:, 0:cw]).then_inc(s_ost[ci % 2], 16)

                    for k, (ci, c0, cw) in enumerate(chl):
                        if k >= 2:
                            store(*chl[k - 2])
                        if ci >= 2:
                            gp.wait_ge(s_F, ci - 1)
                        gp.dma_start(out=DEN[ci % 2][:, 0:cw], in_=ACC[64:128, c0:c0 + cw]).then_inc(s_den[ci % 2], 16)
                    for k in range(max(0, len(chl) - 2), len(chl)):
                        store(*chl[k])
                    if TH + 2 < len(ths):
                        gp.memset(ACC[:, 0:NQMAX], 0.0).then_inc(s_zp, 1)
                for k in range(2):
                    gp.wait_ge(s_ost[k], 16 * ((nch - k + 1) // 2))

    def phase_D():
        TT = 256
        ntile = NQT // TT
        def xrow(n):
            for t in range(NT):
                if n < QOFF[t] + NQS[t]:
                    return t * S + (n - QOFF[t])
            raise AssertionError
        with ExitStack() as es:
            e = es.enter_context
            wo = e(nc.sbuf_tensor("d_wo", [128, 8, DM], BF16))
            wu = e(nc.sbuf_tensor("d_wu", [128, 8, DFF], BF16))
            wd = e(nc.sbuf_tensor("d_wd", [128, 32, DM], BF16))
            xt = [e(nc.sbuf_tensor(f"d_xt{i}", [128, 2, DM], F32)) for i in range(2)]
            mt = [e(nc.sbuf_tensor(f"d_mt{i}", [128, 8, TT], BF16)) for i in range(2)]
            sd = e(nc.sbuf_tensor("d_sd", [128, 5, TT], F32))
            rs = sd
            mixn = e(nc.sbuf_tensor("d_mixn", [128, 8, TT], BF16))
            sqm = mixn
            gm = e(nc.sbuf_tensor("d_gm", [128, 8], F32))
            gmlp = e(nc.sbuf_tensor("d_gmlp", [128, DM], F32))
            gfin = e(nc.sbuf_tensor("d_gfin", [128, DM], F32))
            ssq = e(nc.sbuf_tensor("d_ssq", [128, 4], F32))
            std = e(nc.sbuf_tensor("d_std", [128, 4], F32))
            rstd = e(nc.sbuf_tensor("d_rstd", [128, 4], F32))
            h2 = e(nc.sbuf_tensor("d_h2", [128, 2, DM], BF16))
            junk = h2[:, 0, :]
            h2T = e(nc.sbuf_tensor("d_h2T", [128, 8, TT], BF16))
            rl = [e(nc.sbuf_tensor(f"d_rl{i}", [128, TT], F32)) for i in range(2)]
            aT = [e(nc.sbuf_tensor(f"d_aT{i}", [128, TT], BF16)) for i in range(4)]
            ones = e(nc.sbuf_tensor("d_ones", [128, 128], BF16))
            ident = e(nc.sbuf_tensor("d_ident", [128, 128], BF16))
            yb = [[e(nc.psum_tensor(f"d_yb{j}{hf}", [128, 512], F32)) for hf in range(2)] for j in range(2)]
            ub = [e(nc.psum_tensor(f"d_ub{i}", [128, 512], F32)) for i in range(2)]
            tp = [e(nc.psum_tensor(f"d_tp{i}", [128, 512], BF16)) for i in range(2)]
            S_ = lambda n: e(nc.semaphore("d_" + n))
            s_c = S_("c"); s_w = [S_("w0"), S_("w1")]; s_wc = [S_("wc0"), S_("wc1")]; s_g = S_("g")
            s_xm = [S_("xm0"), S_("xm1")]; s_dsq = S_("dsq"); s_pen = S_("pen"); s_asd = S_("asd"); s_aln = S_("aln")
            s_dmx = S_("dmx"); s_peo = S_("peo"); s_dx1 = S_("dx1"); s_a1 = S_("a1"); s_as2 = S_("as2")
            s_dr2 = S_("dr2"); s_dh2 = S_("dh2"); s_pet = S_("pet"); s_at = S_("at")
            s_peu = S_("peu"); s_ar = S_("ar"); s_da = S_("da"); s_ped = S_("ped")
            s_dx2 = S_("dx2"); s_as3 = S_("as3"); s_dr3 = S_("dr3"); s_dout = S_("dout")
            s_st = [S_("st0"), S_("st1")]
            block = e(nc.Block())

            jobs = []
            for q in range(4):
                jobs.append((w_out[q * 256:(q + 1) * 256, :].rearrange("(c p) n -> p c n", p=128), wo[:, q * 2:(q + 1) * 2, :], True))
            for c in range(8):
                for hf in range(2):
                    jobs.append((w_up[c * 128:(c + 1) * 128, hf * 2048:(hf + 1) * 2048], wu[:, c, hf * 2048:(hf + 1) * 2048], False))
            for q in range(16):
                jobs.append((w_down[q * 256:(q + 1) * 256, :].rearrange("(c p) n -> p c n", p=128), wd[:, q * 2:(q + 1) * 2, :], True))
            NJ = len(jobs)

            def stview(k, shp):
                if shp:
                    return xt[k][:]
                return xt[k][:].rearrange("p c n -> p (c n)")

            @block.sync
            def _(sp):
                sp.dma_start(out=gmlp[:], in_=bcast(g_mlp_t, DM)).then_inc(s_c, 16)
                sp.dma_start(out=gfin[:], in_=bcast(g_fin_t, DM)).then_inc(s_c, 16)
                sp.dma_start(out=ident[:], in_=ident_d).then_inc(s_c, 16)
                sp.dma_start(out=gm[:, 0:4], in_=bass.AP(g_diff_t, 0, [[1, 128], [128, 4]]), allow_slow_non_contiguous=True).then_inc(s_c, 16)
                sp.dma_start(out=gm[:, 4:8], in_=bass.AP(g_dil_t, 0, [[1, 128], [128, 4]]), allow_slow_non_contiguous=True).then_inc(s_c, 16)
                for k, (src, dst, shp) in enumerate(jobs):
                    if k >= 2:
                        sp.wait_ge(s_wc[k % 2], k // 2)
                    sp.dma_start(out=stview(k % 2, shp), in_=src).then_inc(s_w[k % 2], 16)
                sp.wait_ge(s_wc[0], (NJ + 1) // 2)
                sp.wait_ge(s_wc[1], NJ // 2)
                for i in range(ntile):
                    p = i % 2
                    if i >= 2:
                        sp.wait_ge(s_st[p], 16 * (i // 2))
                        sp.wait_ge(s_dmx, i - 1)
                    r = xrow(i * TT)
                    sp.dma_start(out=xt[p][:], in_=x[r:r + TT, :].rearrange("(j p) f -> p j f", p=128)).then_inc(s_xm[p], 16)
                    sp.dma_start(out=mt[p][:], in_=mixT[:, i * TT:(i + 1) * TT].rearrange("(c p) n -> p c n", p=128)).then_inc(s_xm[p], 16)

            @block.scalar
            def _(act):
                for k, (src, dst, shp) in enumerate(jobs):
                    if k % 2 != 0:
                        continue
                    act.wait_ge(s_w[0], 16 * (k // 2 + 1))
                    act.activation(out=dst, in_=stview(0, shp), func=AF.Copy).then_inc(s_wc[0], 1)
                na1 = 0

                def lnexp(i2):
                    act.wait_ge(s_pen, i2 + 1)
                    act.activation(out=sd[:, 0:2, :], in_=yb[0][0][:].rearrange("p (c n) -> p c n", c=2), func=AF.Ln, bias=EPS, scale=1.0 / 128)
                    act.activation(out=sd[:, 2:4, :], in_=yb[0][1][:].rearrange("p (c n) -> p c n", c=2), func=AF.Ln, bias=EPS, scale=1.0 / 128)
                    act.activation(out=sd[:, 4, :], in_=yb[1][0][:, 0:TT], func=AF.Ln, bias=EPS, scale=1.0 / 512).then_inc(s_aln, 1)
                    act.wait_ge(s_aln, i2 + 1)
                    act.activation(out=sd[:], in_=sd[:], func=AF.Exp, scale=-0.5).then_inc(s_asd, 1)

                lnexp(0)
                for i in range(ntile):
                    p = i % 2
                    act.wait_ge(s_dx1, i + 1)
                    for j in range(2):
                        act.activation(out=junk, in_=xt[p][:, j, :], func=AF.Square, accum_out=ssq[:, j:j + 1]).then_inc(s_a1, 1)
                        na1 += 1
                        act.wait_ge(s_a1, na1)
                    act.activation(out=std[:, 0:2], in_=ssq[:, 0:2], func=AF.Sqrt, bias=EPS, scale=1.0 / DM).then_inc(s_as2, 1)
                    for c in range(8):
                        act.wait_ge(s_pet, 8 * i + c + 1)
                        if c == 0 and i >= 1:
                            act.wait_ge(s_peu, 32 * i)
                        act.activation(out=h2T[:, c, :], in_=tp[c % 2][:, 0:TT], func=AF.Copy).then_inc(s_at, 1)
                    if i + 1 < ntile:
                        lnexp(i + 1)
                    for f in range(32):
                        F = 32 * i + f
                        act.wait_ge(s_peu, F + 1)
                        if F >= 2:
                            act.wait_ge(s_da, F - 1)
                        act.activation(out=rl[F % 2][:], in_=ub[F % 2][:, 0:TT], func=AF.Relu).then_inc(s_ar, 1)
                    act.wait_ge(s_dx2, i + 1)
                    for j in range(2):
                        act.activation(out=junk, in_=xt[p][:, j, :], func=AF.Square, accum_out=ssq[:, 2 + j:3 + j]).then_inc(s_a1, 1)
                        na1 += 1
                        act.wait_ge(s_a1, na1)
                    act.activation(out=std[:, 2:4], in_=ssq[:, 2:4], func=AF.Sqrt, bias=EPS, scale=1.0 / DM).then_inc(s_as3, 1)

            @block.vector
            def _(v):
                v.memset(ones[:], 1.0).then_inc(s_g, 1)
                for k, (src, dst, shp) in enumerate(jobs):
                    if k % 2 != 1:
                        continue
                    v.wait_ge(s_w[1], 16 * (k // 2 + 1))
                    v.tensor_copy(out=dst, in_=stview(1, shp)).then_inc(s_wc[1], 1)
                v.wait_ge(s_c, 80)
                v.tensor_scalar(out=gm[:, 0:4], in0=gm[:, 0:4], scalar1=1.0 - LAM_INIT, scalar2=None, op0=ALU.mult).then_inc(s_g, 1)
                v.wait_ge(s_g, 2)
                def sqm_(i2):
                    p2 = i2 % 2
                    v.wait_ge(s_xm[p2], 32 * (i2 // 2 + 1))
                    if i2 >= 1:
                        v.wait_ge(s_peo, i2)
                    v.tensor_tensor(out=sqm[:], in0=mt[p2][:], in1=mt[p2][:], op=ALU.mult).then_inc(s_dsq, 1)

                def mixn_(i2, c):
                    p2 = i2 % 2
                    if c == 0:
                        v.wait_ge(s_asd, i2 + 1)
                    ins = v.scalar_tensor_tensor(out=mixn[:, c, :], in0=mt[p2][:, c, :], scalar=gm[:, c:c + 1],
                                                 in1=rs[:, min(c, 4), :], op0=ALU.mult, op1=ALU.mult)
                    if c == 7:
                        ins.then_inc(s_dmx, 1)

                sqm_(0)
                for c in range(8):
                    mixn_(0, c)
                for i in range(ntile):
                    p = i % 2
                    v.wait_ge(s_peo, i + 1)
                    for j in range(2):
                        for hf in range(2):
                            ins = v.tensor_tensor(out=xt[p][:, j, hf * 512:(hf + 1) * 512], in0=yb[j][hf][:],
                                                  in1=xt[p][:, j, hf * 512:(hf + 1) * 512], op=ALU.add)
                    ins.then_inc(s_dx1, 1)
                    v.wait_ge(s_as2, i + 1)
                    v.reciprocal(out=rstd[:, 0:2], in_=std[:, 0:2]).then_inc(s_dr2, 1)
                    v.wait_ge(s_dr2, i + 1)
                    if i >= 1:
                        v.wait_ge(s_pet, 8 * i)
                    for j in range(2):
                        ins = v.scalar_tensor_tensor(out=h2[:, j, :], in0=xt[p][:, j, :], scalar=rstd[:, j:j + 1],
                                                     in1=gmlp[:], op0=ALU.mult, op1=ALU.mult)
                    ins.then_inc(s_dh2, 1)
                    if i + 1 < ntile:
                        sqm_(i + 1)
                    for f in range(32):
                        F = 32 * i + f
                        v.wait_ge(s_ar, F + 1)
                        if F >= 4:
                            v.wait_ge(s_ped, F - 3)
                        v.tensor_tensor(out=aT[F % 4][:], in0=rl[F % 2][:], in1=rl[F % 2][:], op=ALU.mult).then_inc(s_da, 1)
                        if i + 1 < ntile and 6 <= f < 14:
                            mixn_(i + 1, f - 6)
                    v.wait_ge(s_ped, 32 * (i + 1))
                    for j in range(2):
                        for hf in range(2):
                            ins = v.tensor_tensor(out=xt[p][:, j, hf * 512:(hf + 1) * 512], in0=yb[j][hf][:],
                                                  in1=xt[p][:, j, hf * 512:(hf + 1) * 512], op=ALU.add)
                    ins.then_inc(s_dx2, 1)
                    v.wait_ge(s_as3, i + 1)
                    v.reciprocal(out=rstd[:, 2:4], in_=std[:, 2:4]).then_inc(s_dr3, 1)
                    v.wait_ge(s_dr3, i + 1)
                    for j in range(2):
                        ins = v.scalar_tensor_tensor(out=xt[p][:, j, :], in0=xt[p][:, j, :], scalar=rstd[:, 2 + j:3 + j],
                                                     in1=gfin[:], op0=ALU.mult, op1=ALU.mult)
                    ins.then_inc(s_dout, 1)

            @block.tensor
            def _(pe):
                pe.wait_ge(s_wc[0], (NJ + 1) // 2)
                pe.wait_ge(s_wc[1], NJ // 2)
                pe.wait_ge(s_g, 1)
                pe.wait_ge(s_c, 80)
                def normmm(i2):
                    pe.wait_ge(s_dsq, i2 + 1)
                    for c in range(4):
                        pe.matmul(yb[0][c // 2][:, (c % 2) * TT:(c % 2 + 1) * TT], ones[:], sqm[:, c, :], start=True, stop=True)
                    for c in range(4, 8):
                        ins = pe.matmul(yb[1][0][:, 0:TT], ones[:], sqm[:, c, :], start=(c == 4), stop=(c == 7))
                    ins.then_inc(s_pen, 1)

                normmm(0)
                for i in range(ntile):
                    pe.wait_ge(s_dmx, i + 1)
                    if i >= 1:
                        pe.wait_ge(s_dx2, i)
                    for j in range(2):
                        for hf in range(2):
                            for c in range(8):
                                ins = pe.matmul(yb[j][hf][:], mixn[:, c, j * 128:(j + 1) * 128], wo[:, c, hf * 512:(hf + 1) * 512],
                                                start=(c == 0), stop=(c == 7))
                    ins.then_inc(s_peo, 1)
                    pe.wait_ge(s_dh2, i + 1)
                    for c in range(8):
                        T = 8 * i + c
                        if T >= 2:
                            pe.wait_ge(s_at, T - 1)
                        for j in range(2):
                            ins = pe.transpose(tp[c % 2][:, j * 128:(j + 1) * 128], h2[:, j, c * 128:(c + 1) * 128], ident[:])
                        ins.then_inc(s_pet, 1)
                    pe.wait_ge(s_dx1, i + 1)
                    if i + 1 < ntile:
                        normmm(i + 1)
                    pe.wait_ge(s_at, 8 * (i + 1))

                    def up(f):
                        F = 32 * i + f
                        if F >= 2:
                            pe.wait_ge(s_ar, F - 1)
                        for c in range(8):
                            ins = pe.matmul(ub[F % 2][:, 0:TT], wu[:, c, f * 128:(f + 1) * 128], h2T[:, c, :], start=(c == 0), stop=(c == 7))
                        ins.then_inc(s_peu, 1)

                    def down(f):
                        F = 32 * i + f
                        pe.wait_ge(s_da, F + 1)
                        if f == 0 and i + 1 < ntile:
                            pe.wait_ge(s_asd, i + 2)
                        for j in range(2):
                            for hf in range(2):
                                ins = pe.matmul(yb[j][hf][:], aT[F % 4][:, j * 128:(j + 1) * 128], wd[:, f, hf * 512:(hf + 1) * 512],
                                                start=(f == 0), stop=(f == 31))
                        ins.then_inc(s_ped, 1)

                    up(0)
                    for f in range(32):
                        if f + 1 < 32:
                            up(f + 1)
                        down(f)

            @block.gpsimd
            def _(g):
                for i in range(ntile):
                    g.wait_ge(s_dout, i + 1)
                    g.dma_start(out=y[i * TT:(i + 1) * TT, :].rearrange("(j p) f -> p j f", p=128), in_=xt[i % 2][:]).then_inc(s_st[i % 2], 16)
                for k in range(2):
                    g.wait_ge(s_st[k], 16 * ((ntile - k + 1) // 2))

    if "A" in phases:
        phase_A()
    if "B" in phases:
        phase_B()
    if "C" in phases:
        phase_C()
    if "D" in phases:
        phase_D()
    return nc


def core_inputs(S, NQS, xs, weights, consts):
    m = {"x": np.ascontiguousarray(np.concatenate(xs, axis=0), dtype=np.float32)}
    m.update(weights)
    m.update(consts)
    return m


def prep_weights(g_mix, w_in, lam_qk, g_diff, g_dil, w_out, g_mlp, w_up, w_down, g_final):
    f = lambda a, shp: np.ascontiguousarray(np.asarray(a, dtype=np.float32).reshape(shp))
    return {
        "w_in": f(w_in, (DM, INW)), "w_out": f(w_out, (DM, DM)), "w_up": f(w_up, (DM, DFF)), "w_down": f(w_down, (DFF, DM)),
        "g_mix": f(g_mix, (1, DM)), "g_mlp": f(g_mlp, (1, DM)), "g_final": f(g_final, (1, DM)),
        "g_diff": f(g_diff, (4, 128)), "g_dil": f(g_dil, (1, 512)), "lam_qk": f(lam_qk, (1, 256)),
    }


def kernel(x_prompt, x_sample, g_mix, w_in, lam_qk, g_diff, g_dil, w_out, g_mlp, w_up, w_down, g_final):
    S = SEQ
    NQS = [S, S // 2]
    xp = np.asarray(x_prompt, dtype=np.float32)
    xs = np.asarray(x_sample, dtype=np.float32)
    seqs = [xp[i] for i in range(xp.shape[0])] + [xs[i] for i in range(xs.shape[0])]
    assert len(seqs) == 12
    weights = prep_weights(g_mix, w_in, lam_qk, g_diff, g_dil, w_out, g_mlp, w_up, w_down, g_final)
    consts = host_consts(S)
    in_maps = []
    for c in range(N_CORES):
        s1 = seqs[8 + c // 2]
        if c % 2 == 1:
            s1 = s1[::-1]
        in_maps.append(core_inputs(S, NQS, [seqs[c], s1], weights, consts))
    nc = build_program(S, NQS)
    res = run_bass_kernel_spmd(nc, in_maps, core_ids=list(range(N_CORES)))
    outs = [np.empty((S, DM), np.float32) for _ in range(12)]
    for c in range(N_CORES):
        yc = np.asarray(res.results[c]["y"])
        outs[c][:] = yc[0:S]
        half = yc[S:S + S // 2]
        if c % 2 == 0:
            outs[8 + c // 2][0:S // 2] = half
        else:
            outs[8 + c // 2][S // 2:] = half[::-1]
    y_prompt = np.stack(outs[0:4], axis=0)
    y_sample = np.stack(outs[4:12], axis=0)
    return (y_prompt, y_sample)
```
